# Optimizing a Trainium2 kernel written in Bass

```python
import jax, jax.numpy as jnp
from jax import lax
import numpy as np

D_MODEL = 1024
BATCH = 8
SEQ = 2048
DEPTH = 2
DEC_BATCH = 128
DEC_SEQ = 4
PAST_LEN = 16384
PAGE_SIZE = 128

D_PLE = 256
EPS = 1e-6
A_GROUPS = 4
D_A = D_MODEL // 2
A_CH = D_A // A_GROUPS
CHUNK = 128
D_B = D_MODEL // 2
CONV_W = 3
C_GROUPS = 4
D_C = D_MODEL // 2
C_CH = D_C // C_GROUPS
POOL_WINDOWS = (2, 4, 8, 16)
MAX_WIN = 16
N_BRANCH = 3
SPLITS = (D_A, 2 * D_A, 2 * D_A + D_B, 2 * D_A + 2 * D_B, 2 * D_A + 3 * D_B, 2 * D_A + 3 * D_B + D_C)
D_IN_TOTAL = 2 * D_A + 3 * D_B + D_C + N_BRANCH * D_MODEL
D_FF = 2816

kernel_name = 'hybrid_chunkmlp_conv_pool_decoder_step'


def rmsnorm(x, g):
    xf = x.astype(jnp.float32)
    y = xf * lax.rsqrt(jnp.mean(xf * xf, axis=-1, keepdims=True) + EPS)
    return (y * g.astype(jnp.float32)).astype(x.dtype)


def layernorm(x, g, b):
    xf = x.astype(jnp.float32)
    mu = jnp.mean(xf, axis=-1, keepdims=True)
    var = jnp.mean(jnp.square(xf - mu), axis=-1, keepdims=True)
    y = (xf - mu) * lax.rsqrt(var + EPS)
    return (y * g.astype(jnp.float32) + b.astype(jnp.float32)).astype(x.dtype)


def swiglu(h, w_up, w_down):
    gate, up = jnp.split(h @ w_up, 2, axis=-1)
    return (jax.nn.silu(gate) * up) @ w_down


def chunk_spatial_gate(u, v, w_s, b_s):
    n, t, _ = v.shape
    n_chunks = -(-t // CHUNK)
    pad = n_chunks * CHUNK - t
    vp = jnp.pad(v, ((0, 0), (0, pad), (0, 0))).reshape(n, n_chunks, CHUNK, A_GROUPS, A_CH)
    mask = jnp.tril(jnp.ones((CHUNK, CHUNK), dtype=bool))
    w = jnp.where(mask[None], w_s, 0.0).astype(v.dtype)
    mixed = jnp.einsum('gts,bnsgc->bntgc', w, vp) + b_s.T[None, None, :, :, None]
    mixed = mixed.reshape(n, n_chunks * CHUNK, D_A)[:, :t]
    return u * mixed


def short_conv(xc, buf, w_conv):
    t = xc.shape[1]
    xp = jnp.concatenate([buf, xc], axis=1)
    y = xp[:, 0:t] * w_conv[0]
    for k in range(1, CONV_W):
        y = y + xp[:, k:k + t] * w_conv[k]
    return y, xp[:, -(CONV_W - 1):]


def multiscale_pool(xq, buf, start):
    t = xq.shape[1]
    lb = MAX_WIN - 1
    xcat = jnp.concatenate([buf, xq], axis=1)
    xf = xcat.astype(jnp.float32)
    cs = jnp.pad(jnp.cumsum(xf, axis=1), ((0, 0), (1, 0), (0, 0)))
    pos = start + jnp.arange(t)
    outs = []
    for gi, win in enumerate(POOL_WINDOWS):
        sl = slice(gi * C_CH, (gi + 1) * C_CH)
        hi = cs[:, lb + 1:lb + 1 + t, sl]
        lo = cs[:, lb + 1 - win:lb + 1 - win + t, sl]
        cnt = jnp.minimum(win, pos + 1).astype(jnp.float32)[None, :, None]
        outs.append((hi - lo) / cnt)
    mean = jnp.concatenate(outs, axis=-1)
    return (mean - xf[:, lb:]).astype(xq.dtype), xcat[:, -lb:]


def trunk_layer(x, p, conv_buf, pool_buf, start, w):
    h = rmsnorm(x, w['g_ffn1'])
    x = x + 0.5 * swiglu(h, w['w_ffn1_up'], w['w_ffn1_down'])
    h = rmsnorm(x, w['g_mix'])
    z = h @ w['w_in']
    u_a, v_a, b_gate, c_gate, x_b, x_c, gates = jnp.split(z, SPLITS, axis=-1)
    u_a = jax.nn.gelu(u_a)
    v_a = layernorm(jax.nn.gelu(v_a), w['a_ln_g'], w['a_ln_b'])
    y_a = chunk_spatial_gate(u_a, v_a, w['a_ws'], w['a_bs']) @ w['a_out']
    conv_out, new_conv = short_conv(c_gate * x_b, conv_buf, w['b_conv'])
    y_b = (b_gate * conv_out) @ w['b_out']
    pooled, new_pool = multiscale_pool(x_c, pool_buf, start)
    n, t, _ = pooled.shape
    pc = jnp.einsum('btgc,gcd->btgd', pooled.reshape(n, t, C_GROUPS, C_CH), w['c_w']).reshape(n, t, D_C)
    y_c = (pc * w['c_scale']) @ w['c_out']
    g = jax.nn.sigmoid(gates).reshape(n, t, N_BRANCH, D_MODEL)
    merged = g[:, :, 0] * y_a + g[:, :, 1] * y_b + g[:, :, 2] * y_c
    x = x + merged @ w['w_o']
    h = rmsnorm(x, w['g_ffn2'])
    x = x + 0.5 * swiglu(h, w['w_ffn2_up'], w['w_ffn2_down'])
    h = rmsnorm(x, w['g_ple'])
    x = x + jax.nn.sigmoid(h @ w['w_ple_gate']) * (p @ w['w_ple_proj'])
    return x, v_a, new_conv, new_pool


def setup_inputs(seed: int = 0) -> dict:
    key = jax.random.key(seed)
    ks = iter(jax.random.split(key, 48))

    def nrm(shape, scale):
        return jax.random.normal(next(ks), shape, jnp.float32) * scale

    def gain(shape):
        return 1.0 + nrm(shape, 0.05)

    L = DEPTH
    return {
        'x_prompt': nrm((BATCH, SEQ, D_MODEL), 1.0),
        'x_sample': nrm((DEC_BATCH, DEC_SEQ, D_MODEL), 1.0),
        'state_conv': nrm((L, DEC_BATCH, CONV_W - 1, D_B), 1.0),
        'state_pool': nrm((L, DEC_BATCH, MAX_WIN - 1, D_C), 1.0),
        'p_prompt': nrm((L, BATCH, SEQ, D_PLE), 1.0),
        'p_sample': nrm((L, DEC_BATCH, DEC_SEQ, D_PLE), 1.0),
        'g_ffn1': gain((L, D_MODEL)),
        'w_ffn1_up': nrm((L, D_MODEL, 2 * D_FF), D_MODEL ** -0.5),
        'w_ffn1_down': nrm((L, D_FF, D_MODEL), D_FF ** -0.5),
        'g_mix': gain((L, D_MODEL)),
        'w_in': nrm((L, D_MODEL, D_IN_TOTAL), D_MODEL ** -0.5),
        'a_ln_g': gain((L, D_A)),
        'a_ln_b': nrm((L, D_A), 0.02),
        'a_ws': nrm((L, A_GROUPS, CHUNK, CHUNK), 0.5 * CHUNK ** -0.5),
        'a_bs': 1.0 + nrm((L, A_GROUPS, CHUNK), 0.1),
        'a_out': nrm((L, D_A, D_MODEL), D_A ** -0.5),
        'b_conv': nrm((L, CONV_W, D_B), CONV_W ** -0.5),
        'b_out': nrm((L, D_B, D_MODEL), D_B ** -0.5),
        'c_w': nrm((L, C_GROUPS, C_CH, C_CH), C_CH ** -0.5),
        'c_scale': 1.0 + nrm((L, D_C), 0.1),
        'c_out': nrm((L, D_C, D_MODEL), D_C ** -0.5),
        'w_o': nrm((L, D_MODEL, D_MODEL), D_MODEL ** -0.5),
        'g_ffn2': gain((L, D_MODEL)),
        'w_ffn2_up': nrm((L, D_MODEL, 2 * D_FF), D_MODEL ** -0.5),
        'w_ffn2_down': nrm((L, D_FF, D_MODEL), D_FF ** -0.5),
        'g_ple': gain((L, D_MODEL)),
        'w_ple_gate': nrm((L, D_MODEL, D_MODEL), D_MODEL ** -0.5),
        'w_ple_proj': nrm((L, D_PLE, D_MODEL), D_PLE ** -0.5),
        'g_final': gain((D_MODEL,)),
    }


def reference(x_prompt, x_sample, state_conv, state_pool, p_prompt, p_sample,
              g_ffn1, w_ffn1_up, w_ffn1_down, g_mix, w_in, a_ln_g, a_ln_b, a_ws, a_bs, a_out,
              b_conv, b_out, c_w, c_scale, c_out, w_o, g_ffn2, w_ffn2_up, w_ffn2_down,
              g_ple, w_ple_gate, w_ple_proj, g_final):
    yp, ys = x_prompt, x_sample
    nb = x_prompt.shape[0]
    conv_p, conv_s, pool_p, pool_s, va_s = [], [], [], [], []
    for i in range(DEPTH):
        w = {
            'g_ffn1': g_ffn1[i], 'w_ffn1_up': w_ffn1_up[i], 'w_ffn1_down': w_ffn1_down[i],
            'g_mix': g_mix[i], 'w_in': w_in[i], 'a_ln_g': a_ln_g[i], 'a_ln_b': a_ln_b[i],
            'a_ws': a_ws[i], 'a_bs': a_bs[i], 'a_out': a_out[i], 'b_conv': b_conv[i], 'b_out': b_out[i],
            'c_w': c_w[i], 'c_scale': c_scale[i], 'c_out': c_out[i], 'w_o': w_o[i],
            'g_ffn2': g_ffn2[i], 'w_ffn2_up': w_ffn2_up[i], 'w_ffn2_down': w_ffn2_down[i],
            'g_ple': g_ple[i], 'w_ple_gate': w_ple_gate[i], 'w_ple_proj': w_ple_proj[i],
        }
        zero_conv = jnp.zeros((nb, CONV_W - 1, D_B), x_prompt.dtype)
        zero_pool = jnp.zeros((nb, MAX_WIN - 1, D_C), x_prompt.dtype)
        yp, _, nc_p, np_p = trunk_layer(yp, p_prompt[i], zero_conv, zero_pool, 0, w)
        ys, v_s, nc_s, np_s = trunk_layer(ys, p_sample[i], state_conv[i], state_pool[i], PAST_LEN, w)
        conv_p.append(nc_p)
        conv_s.append(nc_s)
        pool_p.append(np_p)
        pool_s.append(np_s)
        va_s.append(v_s)
    y_prompt = rmsnorm(yp, g_final)
    y_sample = rmsnorm(ys, g_final)
    return (y_prompt, y_sample, jnp.stack(conv_p), jnp.stack(conv_s), jnp.stack(pool_p), jnp.stack(pool_s), jnp.stack(va_s))
```

```python
import numpy as np
import os
KDEBUG = os.environ.get('KDEBUG', '') == '1'
from contextlib import ExitStack
import concourse.bass as bass
import concourse.mybir as mybir
from concourse.bass_utils import run_bass_kernel_spmd

F32 = mybir.dt.float32
BF16 = mybir.dt.bfloat16
AF = mybir.ActivationFunctionType
ALU = mybir.AluOpType

L = 2
D = 1024
DFF = 2816
NF = DFF // 128
DIN = 6144
EPS = 1e-6
NSLOT = 4
SLOT_EL = 3072
ENGINES = ("pe", "act", "dve", "pool", "sp")
SELF_SYNC = ("act", "dve", "pool")
NV = 48
WARM_FFN, WARM_MIX, WARM_PLE = 0, 0, 0
WARM_CONV, WARM_NSTAT, WARM_XLD = 0, 30, 15
EARLY_NORM = True


class Op:
    __slots__ = ("eng", "emit", "reads", "writes", "dma_sem", "deps", "signals", "count", "idx", "ndma", "name")


class Gran(tuple):
    pass


def gran(c0, n):
    return Gran(range(c0 // 64, (c0 + n + 63) // 64))


def expand_res(rs):
    rs2 = []
    for r in rs:
        if isinstance(r, tuple) and len(r) == 2 and r[0] == "h":
            rs2.extend(("h", r[1], c) for c in range(8))
        else:
            rs2.append(r)
    out = []
    for r in rs2:
        if isinstance(r, tuple) and any(isinstance(x, Gran) for x in r):
            i = [isinstance(x, Gran) for x in r].index(True)
            for gnum in r[i]:
                out.append(r[:i] + (gnum,) + r[i + 1:])
        else:
            out.append(r)
    return out


class Sched:
    def __init__(self):
        self.ops = []
        self.last_writer = {}
        self.readers = {}
        self.dma_counts = {}

    def add(self, eng, emit, reads=(), writes=(), dma_sem=None, ndma=1, name=""):
        op = Op()
        op.eng, op.emit, op.reads, op.writes = eng, emit, tuple(expand_res(reads)), tuple(expand_res(writes))
        op.dma_sem, op.ndma, op.name = dma_sem, ndma, name
        op.signals, op.count = False, None
        op.idx = len(self.ops)
        deps = set()
        for r in op.reads:
            w = self.last_writer.get(r)
            if w is not None:
                deps.add(w)
        for w_ in op.writes:
            w = self.last_writer.get(w_)
            if w is not None:
                deps.add(w)
            deps.update(self.readers.get(w_, ()))
        deps.discard(op.idx)
        op.deps = sorted(deps)
        for r in op.reads:
            self.readers.setdefault(r, []).append(op.idx)
        for w_ in op.writes:
            self.last_writer[w_] = op.idx
            self.readers[w_] = []
        self.ops.append(op)
        return op

    def finalize(self):
        ops = self.ops
        for op in ops:
            for d in op.deps:
                dop = ops[d]
                if dop.dma_sem is None and (dop.eng != op.eng or dop.eng in SELF_SYNC):
                    dop.signals = True
        cnt = {e: 0 for e in ENGINES}
        for op in ops:
            if op.dma_sem is not None:
                c = self.dma_counts.get(op.dma_sem, 0) + 16 * op.ndma
                self.dma_counts[op.dma_sem] = c
                op.count = c
            elif op.signals:
                cnt[op.eng] += 1
                op.count = cnt[op.eng]

    def runner(self, sems, dma_sems):
        ops = self.ops
        per = {e: [] for e in ENGINES}
        for op in ops:
            per[op.eng].append(op)

        def run(ename, e):
            waited = {}
            for op in per[ename]:
                need = {}
                for d in op.deps:
                    dop = ops[d]
                    if dop.dma_sem is not None:
                        key = ("dma", dop.dma_sem)
                    elif dop.eng == ename and ename not in SELF_SYNC:
                        continue
                    else:
                        key = ("eng", dop.eng)
                    if need.get(key, 0) < dop.count:
                        need[key] = dop.count
                for key, c in need.items():
                    if waited.get(key, 0) >= c:
                        continue
                    waited[key] = c
                    e.wait_ge(dma_sems[key[1]] if key[0] == "dma" else sems[key[1]], c)
                ins = op.emit(e)
                if op.dma_sem is not None:
                    lst = ins if isinstance(ins, (list, tuple)) else [ins]
                    assert len(lst) == op.ndma, (op.name, len(lst), op.ndma)
                    for i_ in lst:
                        i_.then_inc(dma_sems[op.dma_sem], 16)
                elif op.signals:
                    ins.then_inc(sems[ename], 1)
        return run


def build_nc():
    nc = bass.Bass("TRN2", target_bir_lowering=False)
    S = Sched()

    def din(name, shape):
        return nc.dram_tensor(name, list(shape), F32, kind="ExternalInput").ap()

    def dout(name, shape):
        return nc.dram_tensor(name, list(shape), F32, kind="ExternalOutput").ap()

    xp = din("xp", [2048, D]); xs = din("xs", [64, D])
    sconv = din("sconv", [L, 32, 512]); spool = din("spool", [L, 240, 512])
    pp = din("pp", [L, 2048, 256]); ps_ = din("ps", [L, 64, 256])
    vecs = din("vecs", [128, L * NV + 8])
    w_up = [din("w_ffn1_up", [L, D, 2 * DFF]), din("w_ffn2_up", [L, D, 2 * DFF])]
    w_dn = [din("w_ffn1_down", [L, DFF, D]), din("w_ffn2_down", [L, DFF, D])]
    w_in = din("w_in", [L, D, DIN])
    a_out = din("a_out", [L, 512, D]); b_out = din("b_out", [L, 512, D]); c_out = din("c_out", [L, 512, D])
    w_o = din("w_o", [L, D, D]); w_pg = din("w_ple_gate", [L, D, D]); w_pp = din("w_ple_proj", [L, 256, D])
    c_w = din("c_w", [L, 4, 128, 128])
    wsT_d = din("wsT", [L, 4, 128, 128]); bd_d = din("bd", [L, 4, 64, 64])
    bias4_d = din("bias4", [L, 4 * 128]); biass_d = din("biass", [L, 4 * 64])
    lng_d = din("a_ln_g", [L, 512]); lnb_d = din("a_ln_b", [L, 512])

    yp = dout("yp", [2048, D]); ys = dout("ys", [64, D])
    conv_p = dout("conv_p", [L, 2, 512]); conv_s = dout("conv_s", [L, 32, 512])
    pool_p = dout("pool_p", [L, 15, 512]); pool_s = dout("pool_s", [L, 240, 512])
    va_s = dout("va_s", [L, 64, 512])
    if KDEBUG:
        dbg_x = dout("dbg_x", [128, 8, 64]); dbg_h = nc.dram_tensor("dbg_h", [128, 8, 64], BF16, kind="ExternalOutput").ap()
        dbg_r = nc.dram_tensor("dbg_r", [128, 22, 64], BF16, kind="ExternalOutput").ap()
        dbg_r0 = nc.dram_tensor("dbg_r0", [128, 22, 64], BF16, kind="ExternalOutput").ap()
        dbg_xs = dout("dbg_xs", [8, 128, 8, 64])

    TT = 1088
    with ExitStack() as es:
        def sb(name, shape, dt=F32):
            return es.enter_context(nc.sbuf_tensor("sb_" + name, list(shape), dt))

        xT = sb("xT", [128, 8, TT])
        hT = sb("hT", [128, 8, TT], BF16)
        R = sb("R", [128, 22, TT], BF16)
        vbf = sb("vbf", [128, 9, 512], BF16)
        pT = sb("pT", [128, 2, TT], BF16)
        slots = [sb(f"slot{i}", [128, SLOT_EL], BF16) for i in range(NSLOT)]
        stage = [sb(f"stage{i}", [128, D]) for i in range(2)]
        pstage = [sb(f"pstage{i}", [128, 256]) for i in range(2)]
        tmpA = [sb(f"tmpA{i}", [128, 512]) for i in range(2)]
        tmpB = [sb(f"tmpB{i}", [128, 512]) for i in range(2)]
        msb = [sb(f"ms{i}", [128, 512]) for i in range(1)]
        rstd = [sb(f"rstd{i}", [128, 512]) for i in range(1)]
        cinp = sb("cinp", [128, 2 + 1024])
        cins = sb("cins", [128, 4, 96])
        xcb = sb("xcb", [128, 15 + 1024])
        xcs = sb("xcs", [128, 4, 304])
        lev = [sb(f"lev{i}", [128, 15 + 512]) for i in range(2)]
        pooled = [sb(f"pooled{i}", [128, 512], BF16) for i in range(4)]
        hist_c = sb("hist_c", [128, L, 4, 2])
        hist_p = sb("hist_p", [128, L, 4, 15])
        vs_f = sb("vs_f", [64, 512])
        vstat = sb("vstat", [128, 9, 8])
        vg_s = sb("vg_s", [64, 512])
        vtmp = sb("vtmp", [128, 512])
        ident = sb("ident", [128, 128]); mask = sb("mask", [128, 128])
        ones_bf = sb("ones_bf", [128, 128], BF16)
        nhalf = sb("nhalf", [128, 1]); epsb = sb("epsb", [128, 1])
        invcnt = sb("invcnt", [128, 4, 16])
        vec_sb = sb("vec_sb", [128, L * NV + 8])
        wsT_b = sb("wsT_b", [128, L * 4, 128], BF16)
        bd_b = sb("bd_b", [64, L * 4, 64], BF16)
        cw_b = sb("cw_b", [128, L, 4, 128], BF16)
        bias4 = sb("bias4", [128, L, 4 * 128]); biass = sb("biass", [128, L, 4 * 64])
        lng = sb("lng", [128, L, 512]); lnb = sb("lnb", [128, L, 512])

        banks = [es.enter_context(nc.psum_tensor(f"bank{i}", [128, 512], F32)) for i in range(8)]
        sems = {e: es.enter_context(nc.semaphore("s_" + e)) for e in ENGINES}
        dnames = [f"w{i}" for i in range(NSLOT)] + ["sin0", "sin1", "sout0", "sout1", "pin0", "pin1", "const",
                                                    "cw", "state", "o_small", "o_v", "d2d", "dbg", "dbg2", "dbg3"] + [f"dx{i}" for i in range(8)]
        dsems = {n: es.enter_context(nc.semaphore("d_" + n)) for n in dnames}

        bank_ctr = [0]

        free_banks = [list(range(8))]

        def nb():
            fb = free_banks[0]
            b = fb[bank_ctr[0] % len(fb)]
            bank_ctr[0] += 1
            return b

        rot = {}

        def rotate(key, n):
            i = rot.get(key, 0)
            rot[key] = i + 1
            return i % n

        ones_f = tmpB[0][:, 0:128]
        S.add("pool", lambda e: e.memset(ones_f, 1.0), writes=[("tmpB", 0)], name="c_ones")
        S.add("pool", lambda e: e.affine_select(out=ident[:], in_=ones_f, pattern=[[1, 128]], compare_op=ALU.is_equal,
                                               fill=0.0, base=0, channel_multiplier=-1), reads=[("tmpB", 0)], writes=["ident"], name="c_ident")
        S.add("pool", lambda e: e.affine_select(out=mask[:], in_=ones_f, pattern=[[1, 128]], compare_op=ALU.is_ge,
                                               fill=0.0, base=0, channel_multiplier=-1), reads=[("tmpB", 0)], writes=["mask"], name="c_mask")

        def const_setup(e):
            e.memset(ones_bf[:], 1.0 / 1024.0)
            e.memset(nhalf[:], -0.5)
            e.memset(epsb[:], EPS)
            for j in range(4):
                w = 2 ** (j + 1)
                e.memset(invcnt[:, j, w - 1:16], 1.0 / w)
                for t in range(w - 1):
                    e.memset(invcnt[:, j, t:t + 1], 1.0 / (t + 1))
            e.memset(vstat[:], 1.0)
            return e.memset(hist_c[:], 0.0)
        S.add("pool", const_setup, writes=["ones_bf", "nhalf", "epsb", "invcnt", "hist", "vstat01"], name="consts")

        wsT_f = stage[0][:, :].rearrange("p (g t) -> p g t", t=128)
        bd_f = stage[1][0:64, 0:512].rearrange("p (g t) -> p g t", t=64)

        def const_loads(e):
            r = [e.dma_start(out=vec_sb[:], in_=vecs)]
            r.append(e.dma_start(out=wsT_f, in_=wsT_d.rearrange("l g s t -> s (l g) t")))
            r.append(e.dma_start(out=bd_f, in_=bd_d.rearrange("l g s t -> s (l g) t")))
            for l in range(L):
                r.append(e.dma_start(out=bias4[:, l, :], in_=bias4_d[l].partition_broadcast(128)))
                r.append(e.dma_start(out=biass[:, l, :], in_=biass_d[l].partition_broadcast(128)))
                r.append(e.dma_start(out=lng[:, l, :], in_=lng_d[l].partition_broadcast(128)))
                r.append(e.dma_start(out=lnb[:, l, :], in_=lnb_d[l].partition_broadcast(128)))
            return r
        S.add("sp", const_loads, writes=["vec", ("stage", 0), ("stage", 1), "bias", "ln"], dma_sem="const", ndma=3 + 4 * L, name="const_loads")
        S.add("pool", lambda e: e.dma_start(out=cw_b[:], in_=c_w.rearrange("l g c d -> c l g d")), writes=["cw"], dma_sem="cw", name="cw")

        S.add("dve", lambda e: e.tensor_tensor(out=wsT_f, in0=wsT_f, in1=mask[:, :].unsqueeze(1).to_broadcast([128, L * 4, 128]), op=ALU.mult),
              reads=[("stage", 0), "mask"], writes=[("stage", 0)], name="ws_mask")
        S.add("dve", lambda e: e.tensor_copy(out=wsT_b[:], in_=wsT_f), reads=[("stage", 0)], writes=["wsT_b"], name="ws_cast")
        S.add("dve", lambda e: e.tensor_tensor(out=bd_f, in0=bd_f, in1=mask[0:64, 0:64].unsqueeze(1).to_broadcast([64, L * 4, 64]), op=ALU.mult),
              reads=[("stage", 1), "mask"], writes=[("stage", 1)], name="bd_mask")
        S.add("dve", lambda e: e.tensor_copy(out=bd_b[:], in_=bd_f), reads=[("stage", 1)], writes=["bd_b"], name="bd_cast")

        def vcol(l, off, c):
            o = l * NV + off + c
            return vec_sb[:, o:o + 1]
        G_FFN1, G_MIX, G_FFN2, G_PLE, C_SCALE, B_CONV = 0, 8, 16, 24, 32, 36

        slabs = []
        slab_state = {"next_load": 0, "next_use": 0}

        def w_part(ap2d, kc, c):
            return (ap2d, kc, c)

        def plan_layer(l):
            def ffn(which):
                for f in range(NF):
                    slabs.append([w_part(w_up[which][l, :, f * 128:(f + 1) * 128], 8, 128),
                                  w_part(w_up[which][l, :, DFF + f * 128:DFF + (f + 1) * 128], 8, 128)])
                for d in range(8):
                    slabs.append([w_part(w_dn[which][l, :, d * 128:(d + 1) * 128], NF, 128)])
            ffn(0)
            for s2 in range(2):
                slabs.append([w_part(w_in[l, :, s2 * 256:(s2 + 1) * 256], 8, 256)])
            for s2 in range(2):
                slabs.append([w_part(w_in[l, :, 512 + s2 * 256:512 + (s2 + 1) * 256], 8, 256)])
            for j in range(4):
                slabs.append([w_part(w_in[l, :, 1536 + j * 128:1536 + (j + 1) * 128], 8, 128),
                              w_part(w_in[l, :, 2048 + j * 128:2048 + (j + 1) * 128], 8, 128),
                              w_part(w_in[l, :, 1024 + j * 128:1024 + (j + 1) * 128], 8, 128)])
            for s2 in range(2):
                slabs.append([w_part(w_in[l, :, 2560 + s2 * 256:2560 + (s2 + 1) * 256], 8, 256)])
            for dp in range(4):
                slabs.append([w_part(a_out[l, :, dp * 256:(dp + 1) * 256], 4, 256),
                              w_part(b_out[l, :, dp * 256:(dp + 1) * 256], 4, 256),
                              w_part(c_out[l, :, dp * 256:(dp + 1) * 256], 4, 256)])
                for dd in range(2):
                    d = 2 * dp + dd
                    slabs.append([w_part(w_in[l, :, 3072 + br * 1024 + d * 128:3072 + br * 1024 + (d + 1) * 128], 8, 128)
                                  for br in range(3)])
            for s4 in range(4):
                slabs.append([w_part(w_o[l, :, s4 * 256:(s4 + 1) * 256], 8, 256)])
            ffn(1)
            for hf in range(2):
                slabs.append([w_part(w_pp[l, :, hf * 512:(hf + 1) * 512], 2, 512)])
                for s4 in (2 * hf, 2 * hf + 1):
                    slabs.append([w_part(w_pg[l, :, s4 * 256:(s4 + 1) * 256], 8, 256)])

        for _tile in range(2):
            for l in range(L):
                plan_layer(l)

        slab_slot = {}

        def slab_views(k):
            s = slab_slot[k]
            views, off = [], 0
            for (ap2d, kc, c) in slabs[k]:
                views.append(slots[s][:, off:off + kc * c].rearrange("p (k c) -> p k c", c=c))
                off += kc * c
            assert off <= SLOT_EL
            return views

        def issue_load(s):
            k = slab_state["next_load"]
            if k >= len(slabs):
                return
            slab_state["next_load"] = k + 1
            slab_slot[k] = s
            views = slab_views(k)
            parts = slabs[k]

            def emit(e, views=views, parts=parts):
                r = []
                for v, (ap2d, kc, c) in zip(views, parts):
                    r.append(e.dma_start(out=v, in_=ap2d.rearrange("(k p) c -> p k c", p=128)))
                return r
            S.add("pool", emit, writes=[("slot", s)], dma_sem=f"w{s}", ndma=len(parts), name=f"wload{k}")

        def consume():
            k = slab_state["next_use"]
            assert k < slab_state["next_load"], ("slab consumed before its load was issued", k)
            slab_state["next_use"] += 1
            return k, slab_views(k), ("slot", slab_slot[k])

        def release(k):
            issue_load(slab_slot[k])

        for s_ in range(NSLOT):
            issue_load(s_)

        def pe_mm(bank_i, out_ap, pairs, reads, name="mm"):
            def emit(e, out_ap=out_ap, pairs=pairs):
                last = None
                n = len(pairs)
                for i, (lt, rh) in enumerate(pairs):
                    last = e.matmul(out_ap, lt, rh, start=(i == 0), stop=(i == n - 1))
                return last
            S.add("pe", emit, reads=reads, writes=[("bank", bank_i)], name=name)

        def pe_multi(bank_i, groups_, reads, name="mmm"):
            def emit(e, groups_=groups_):
                last = None
                for out_ap, pairs in groups_:
                    n = len(pairs)
                    for i, (lt, rh) in enumerate(pairs):
                        last = e.matmul(out_ap, lt, rh, start=(i == 0), stop=(i == n - 1))
                return last
            S.add("pe", emit, reads=reads, writes=[("bank", bank_i)], name=name)

        def warm(n_mm):
            if n_mm <= 0:
                return
            b = free_banks[0][bank_ctr[0] % len(free_banks[0])]

            def emit(e, b=b):
                last = None
                for _ in range(n_mm):
                    last = e.matmul(banks[b][:, 0:128], ones_bf[:], ones_bf[:], start=True, stop=True)
                return last
            S.add("pe", emit, reads=["ones_bf"], writes=[("bank", b)], name="warm")

        def xres(g):
            return [("x", d, g[3]) for d in range(8)]

        stat_bank = {}
        pending_stats = []

        early_norm = [None]
        norm_done = [False]

        pending_en = [None]

        def maybe_early_norm(d, g):
            if d == 7 and early_norm[0] is not None:
                if pending_en[0] is not None:
                    flush_stats(pending_en[0][:2])
                    early_norm[0](pending_en[0])
                pending_en[0] = g

        def end_phase():
            flush_stats()
            if pending_en[0] is not None and early_norm[0] is not None:
                early_norm[0](pending_en[0])
            pending_en[0] = None

        def post_x_update(d, g, scratch):
            c0, n, _, gid = g[:4]
            cs = slice(c0, c0 + n)
            gi = [q[:2] for q in groups_now].index(g[:2])
            sb_i = stat_bank[gi]
            if scratch == "h":
                sq_ap, sq_res = hT[:, d, cs], ("h", gid, d)
            else:
                sq_ap, sq_res = R[:, d, cs], ("R", d, gid)
            S.add("act", lambda e, sq_ap=sq_ap, d=d, cs=cs: e.activation(out=sq_ap, in_=xT[:, d, cs], func=AF.Square),
                  reads=[("x", d, gid)], writes=[sq_res], name="fsq")

            def do_stat(sq_ap=sq_ap, sq_res=sq_res, sb_i=sb_i, n=n, d=d):
                def emit(e):
                    return e.matmul(banks[sb_i][:, :n], ones_bf[:], sq_ap, start=(d == 0), stop=(d == 7))
                S.add("pe", emit, reads=[sq_res, "ones_bf"], writes=[("bank", sb_i)], name="fstat")
            pending_stats.append((g[:2], do_stat))
            if len(pending_stats) > 8:
                pending_stats.pop(0)[1]()

        def flush_stats(gkey=None):
            keep = []
            while pending_stats:
                k_, fn = pending_stats.pop(0)
                if gkey is None or k_ == gkey:
                    fn()
                else:
                    keep.append((k_, fn))
            pending_stats.extend(keep)

        def rmsnorm(g, gcol_fn, final=False, fused=False):
            c0, n, _, gid = g[:4]
            cs = slice(c0, c0 + n)
            if fused:
                b = stat_bank[[q[:2] for q in groups_now].index(g[:2])]
            else:
                def sq(e):
                    last = None
                    for c in range(8):
                        last = e.activation(out=hT[:, c, cs], in_=xT[:, c, cs], func=AF.Square)
                    return last
                S.add("act", sq, reads=xres(g), writes=[("h", gid)], name="nsq")
                warm(WARM_NSTAT)
                b = nb()
                pe_mm(b, banks[b][:, :n], [(ones_bf[:], hT[:, c, cs]) for c in range(8)], [("h", gid), "ones_bf"], "nstat")
            mi = rotate("ms", 1)
            if fused and groups_now and g[:2] == groups_now[0][:2]:
                S.add("act", lambda e: e.activation(out=nhalf[:, 0:1], in_=epsb[:, 0:1], func=AF.Ln), reads=["epsb"], writes=["nhalf_scr"], name="tblpre")
            S.add("act", lambda e: e.activation(out=msb[mi][:, :n], in_=banks[b][:, :n], func=AF.Ln, bias=epsb[:, 0:1], scale=1.0),
                  reads=[("bank", b), "epsb"], writes=[("ms", mi)], name="nln")
            ri = rotate("rstd", 1)
            S.add("act", lambda e: e.activation(out=rstd[ri][:, :n], in_=msb[mi][:, :n], func=AF.Exp, scale=-0.5),
                  reads=[("ms", mi)], writes=[("rstd", ri)], name="nexp")

            def apply(e):
                last = None
                for c in range(8):
                    out = xT[:, c, cs] if final else hT[:, c, cs]
                    last = e.scalar_tensor_tensor(out=out, in0=xT[:, c, cs], scalar=gcol_fn(c), in1=rstd[ri][:, :n],
                                                  op0=ALU.mult, op1=ALU.mult)
                return last
            if final:
                S.add("dve", apply, reads=xres(g) + [("rstd", ri), "vec"], writes=xres(g), name="napply")
            else:
                S.add("dve", apply, reads=xres(g) + [("rstd", ri), "vec"], writes=[("h", gid)], name="napply")

        def ffn(l, which, groups, interleave=(), fused_in=False):
            gbase = G_FFN1 if which == 0 else G_FFN2
            interleave = list(interleave)
            if not norm_done[0]:
                for g in groups:
                    rmsnorm(g, lambda c: vcol(l, gbase, c), fused=fused_in)
            warm(WARM_FFN)
            def up_item(f, g, wg, wu, sres):
                c0, n, _, gid = g[:4]
                cs = slice(c0, c0 + n)
                bg, bu = nb(), nb()
                pe_mm(bg, banks[bg][:, :n], [(wg[:, kk, :], hT[:, kk, cs]) for kk in range(8)], [sres, ("h", gid)], "up_g")
                pe_mm(bu, banks[bu][:, :n], [(wu[:, kk, :], hT[:, kk, cs]) for kk in range(8)], [sres, ("h", gid)], "up_u")
                ti = rotate("tmpA", 2)
                S.add("act", lambda e, ti=ti, bg=bg, n=n: e.activation(out=tmpA[ti][:, :n], in_=banks[bg][:, :n], func=AF.Silu),
                      reads=[("bank", bg)], writes=[("tmpA", ti)], name="silu")
                S.add("dve", lambda e, ti=ti, bu=bu, n=n, f=f, cs=cs: e.tensor_tensor(out=R[:, f, cs], in0=tmpA[ti][:, :n], in1=banks[bu][:, :n], op=ALU.mult),
                      reads=[("tmpA", ti), ("bank", bu)], writes=[("R", f, gid)], name="act")
            HEAD = 2
            head = [consume() for _ in range(HEAD)]
            for g in groups:
                for f in range(HEAD):
                    k, (wg, wu), sres = head[f]
                    up_item(f, g, wg, wu, sres)
            for f in range(HEAD):
                release(head[f][0])
            for f in range(HEAD, NF):
                k, (wg, wu), sres = consume()
                for g in groups:
                    up_item(f, g, wg, wu, sres)
                release(k)
            for d in range(8):
                k, (wd,), sres = consume()
                for g in groups:
                    c0, n, _, gid = g[:4]
                    cs = slice(c0, c0 + n)
                    b = nb()
                    pe_mm(b, banks[b][:, :n], [(wd[:, kk, :], R[:, kk, cs]) for kk in range(NF)],
                          [sres] + [("R", kk, gid) for kk in range(NF)], "down")
                    S.add("dve", lambda e, b=b, n=n, d=d, cs=cs: e.scalar_tensor_tensor(out=xT[:, d, cs], in0=banks[b][:, :n], scalar=0.5, in1=xT[:, d, cs],
                                                                                      op0=ALU.mult, op1=ALU.add),
                          reads=[("bank", b), ("x", d, gid)], writes=[("x", d, gid)], name="res")
                    post_x_update(d, g, "h")
                    maybe_early_norm(d, g)
                release(k)
                if interleave:
                    interleave.pop(0)()
            while interleave:
                interleave.pop(0)()
            end_phase()

        def load_p(l, tile, groups):
            blocks = [(pp[l, tile * 1024 + i * 128: tile * 1024 + (i + 1) * 128, :], i * 128, 128) for i in range(8)]
            if tile == 1:
                blocks.append((ps_[l, :, :], 1024, 64))
            return [(lambda src=src, t0=t0, m=m: load_p_block(src, t0, m)) for (src, t0, m) in blocks]

        def load_p_block(src, t0, m):
            if True:
                pi = rotate("pstage", 2)
                S.add("sp", lambda e, pi=pi, src=src, m=m: e.dma_start(out=pstage[pi][:m, :], in_=src), writes=[("pstage", pi)], dma_sem=f"pin{pi}", name="pload")
                b = nb()

                def tr(e, pi=pi, b=b, m=m):
                    last = None
                    for kk in range(2):
                        last = e.transpose(banks[b][:, kk * 128:kk * 128 + m], pstage[pi][:m, kk * 128:(kk + 1) * 128], ident[:m, :m])
                    return last
                S.add("pe", tr, reads=[("pstage", pi), "ident"], writes=[("bank", b)], name="ptr")
                S.add("act", lambda e, b=b, t0=t0, m=m: e.activation(out=pT[:, :, t0:t0 + m], in_=banks[b][:, 0:256].rearrange("p (k c) -> p k c", c=128)[:, :, :m], func=AF.Copy),
                      reads=[("bank", b)], writes=[("pT", t0)], name="pcopy")

        def ptreads(g):
            c0, n = g[0], g[1]
            return [("pT", t0) for t0 in range(c0, c0 + n, 128)] if n >= 128 else [("pT", c0)]

        def ple(l, tile, groups):
            if not norm_done[0]:
                for g in groups:
                    rmsnorm(g, lambda c: vcol(l, G_PLE, c), fused=True)
            warm(WARM_PLE)
            for s4 in range(4):
                if s4 % 2 == 0:
                    kp, (wp,), pres = consume()
                k, (wg,), sres = consume()
                for g in groups:
                    for dd in range(2):
                        d = 2 * s4 + dd
                        c0, n, _, gid = g[:4]
                        cs = slice(c0, c0 + n)
                        bg, bp = nb(), nb()
                        pe_mm(bg, banks[bg][:, :n], [(wg[:, kk, dd * 128:(dd + 1) * 128], hT[:, kk, cs]) for kk in range(8)], [sres, ("h", gid)], "pg")
                        pe_mm(bp, banks[bp][:, :n], [(wp[:, kk, (d % 4) * 128:(d % 4 + 1) * 128], pT[:, kk, cs]) for kk in range(2)], [pres] + ptreads(g), "pp")
                        ti = rotate("tmpA", 2)
                        S.add("act", lambda e, ti=ti, bg=bg, n=n: e.activation(out=tmpA[ti][:, :n], in_=banks[bg][:, :n], func=AF.Sigmoid),
                              reads=[("bank", bg)], writes=[("tmpA", ti)], name="psig")
                        tb_ = rotate("tmpB", 2)

                        S.add("dve", lambda e, ti=ti, tb_=tb_, bp=bp, n=n: e.tensor_tensor(out=tmpB[tb_][:, :n], in0=tmpA[ti][:, :n], in1=banks[bp][:, :n], op=ALU.mult),
                              reads=[("tmpA", ti), ("bank", bp)], writes=[("tmpB", tb_)], name="pmul")
                        S.add("dve", lambda e, tb_=tb_, n=n, d=d, cs=cs: e.tensor_tensor(out=xT[:, d, cs], in0=xT[:, d, cs], in1=tmpB[tb_][:, :n], op=ALU.add),
                              reads=[("tmpB", tb_), ("x", d, gid)], writes=[("x", d, gid)], name="padd")
                        post_x_update(d, g, "R")
                        maybe_early_norm(d, g)
                release(k)
                if s4 % 2 == 1:
                    release(kp)
            end_phase()

        def load_state(l):
            def ld(e):
                return [e.dma_start(out=stage[0][0:32, 0:512], in_=sconv[l]),
                        e.dma_start(out=stage[0][0:120, 512:1024], in_=spool[l, 0:120, :]),
                        e.dma_start(out=stage[1][0:120, 0:512], in_=spool[l, 120:240, :])]
            S.add("sp", ld, writes=[("stage", 0), ("stage", 1)], dma_sem="state", ndma=3, name="stload")
            for (si, c_lo, m, kind, off) in ((0, 0, 32, "c", 0), (0, 512, 120, "p", 0), (1, 0, 120, "p", 120)):
                b = nb()

                def tr(e, b=b, si=si, c_lo=c_lo, m=m):
                    last = None
                    for j in range(4):
                        last = e.transpose(banks[b][:, j * 128:j * 128 + m], stage[si][:m, c_lo + j * 128:c_lo + (j + 1) * 128], ident[:m, :m])
                    return last
                S.add("pe", tr, reads=[("stage", si), "ident"], writes=[("bank", b)], name="sttr")
                src = banks[b][:, :].rearrange("p (j c) -> p j c", c=128)[:, :, :m]
                if kind == "c":
                    S.add("act", lambda e, src=src: e.activation(out=cins[:, :, 0:32], in_=src, func=AF.Copy),
                          reads=[("bank", b)], writes=["cins_h"], name="stc")
                else:
                    S.add("act", lambda e, src=src, off=off, m=m: e.activation(out=xcs[:, :, off:off + m], in_=src, func=AF.Copy),
                          reads=[("bank", b)], writes=[("xcs_h", off)], name="stp")
            S.add("sp", lambda e: e.dma_start(out=pool_s[l, 0:176, :], in_=spool[l, 64:240, :]), dma_sem="d2d", name="pool_d2d")

        MERGED = [16, 17, 18, 19, 20, 21, 0, 1]

        def mixer(l, tile, groups, mgroups):
            pgroups = [g for g in groups if g[2] == "p"]
            sgroups = [g for g in groups if g[2] == "s"]
            if not norm_done[0]:
                for g in mgroups:
                    rmsnorm(g, lambda c: vcol(l, G_MIX, c), fused=True)
            warm(WARM_MIX)
            if KDEBUG and tile == 1 and l == 0:
                S.add("sp", lambda e: e.dma_start(out=dbg_h, in_=hT[:, :, 1024:1088]), reads=[("h", groups[2][3])], dma_sem="dbg2", name="dbgh")
            if sgroups:
                load_state(l)
            uslabs = [consume(), consume()]
            for g in mgroups:
                c0, n, _, gid = g[:4]
                cs = slice(c0, c0 + n)
                for j in range(4):
                    k, (wu,), sres = uslabs[j // 2]
                    jj = j % 2
                    b = nb()
                    pe_mm(b, banks[b][:, :n], [(wu[:, kk, jj * 128:(jj + 1) * 128], hT[:, kk, cs]) for kk in range(8)], [sres, ("h", gid)], "u")
                    S.add("act", lambda e, b=b, n=n, j=j, cs=cs: e.activation(out=R[:, j, cs], in_=banks[b][:, :n], func=AF.Gelu_apprx_tanh),
                          reads=[("bank", b)], writes=[("R", j, gid)], name="ugelu")
            release(uslabs[0][0])
            release(uslabs[1][0])
            k0, (wv0,), sres0 = consume()
            k1, (wv1,), sres1 = consume()
            blocks = []
            for g in pgroups:
                for i in range(g[1] // 128):
                    blocks.append((g[0] + i * 128, 128, (g[0] + i * 128) // 128, g[3], False))
            for g in sgroups:
                blocks.append((g[0], 64, 8, g[3], True))
            for (t0, m, tb, gid, is_s) in blocks:
                b = nb()
                pe_multi(b, [(banks[b][:m, 0:256], [(hT[:, kk, t0:t0 + m], wv0[:, kk, :]) for kk in range(8)]),
                             (banks[b][:m, 256:512], [(hT[:, kk, t0:t0 + m], wv1[:, kk, :]) for kk in range(8)])],
                         [sres0, sres1, ("h", gid)], "v")
                dst = vg_s[:m, :] if is_s else vbf[:m, tb, :]
                dres = "vg_s" if is_s else ("v", tb)
                ti = rotate("tmpA", 2)
                S.add("act", lambda e, b=b, m=m, dst=dst, tb=tb: e.activation(out=dst, in_=banks[b][:m, :], func=AF.Gelu_apprx_tanh, accum_out=vstat[:m, tb, 0:1]),
                      reads=[("bank", b)], writes=[dres, "vstat01"], name="vgelu")
                S.add("act", lambda e, m=m, dst=dst, tb=tb, ti=ti: e.activation(out=tmpA[ti][:m, :], in_=dst, func=AF.Square, accum_out=vstat[:m, tb, 1:2]),
                      reads=[dres], writes=[("tmpA", ti), "vstat01"], name="vsq")
            release(k0)
            release(k1)
            S.add("dve", lambda e: e.tensor_scalar(out=vstat[:, :, 2], in0=vstat[:, :, 0], scalar1=1.0 / 512.0, scalar2=None, op0=ALU.mult),
                  reads=["vstat01"], writes=["vs2"], name="vmean")
            S.add("dve", lambda e: e.tensor_tensor(out=vstat[:, :, 3], in0=vstat[:, :, 0], in1=vstat[:, :, 0], op=ALU.mult),
                  reads=["vstat01"], writes=["vs3"], name="vs1sq")
            S.add("dve", lambda e: e.tensor_scalar(out=vstat[:, :, 6], in0=vstat[:, :, 3], scalar1=-1.0 / (512.0 * 512.0), scalar2=EPS, op0=ALU.mult, op1=ALU.add),
                  reads=["vs3"], writes=["vs6"], name="vnm")
            S.add("dve", lambda e: e.scalar_tensor_tensor(out=vstat[:, :, 4], in0=vstat[:, :, 1], scalar=1.0 / 512.0, in1=vstat[:, :, 6], op0=ALU.mult, op1=ALU.add),
                  reads=["vstat01", "vs6"], writes=["vs4"], name="vvar")
            S.add("act", lambda e: e.activation(out=vstat[:, :, 7], in_=vstat[:, :, 4], func=AF.Ln), reads=["vs4"], writes=["vs7"], name="vln")
            S.add("act", lambda e: e.activation(out=vstat[:, :, 5], in_=vstat[:, :, 7], func=AF.Exp, scale=-0.5), reads=["vs7"], writes=["vs5"], name="vexp")
            S.add("dve", lambda e: e.scalar_tensor_tensor(out=vstat[:, :, 6], in0=vstat[:, :, 2], scalar=-1.0, in1=vstat[:, :, 5], op0=ALU.mult, op1=ALU.mult),
                  reads=["vs2", "vs5"], writes=["vs6"], name="vnmr")
            v3_pending = []

            def v3_block(t0, m, tb, gid, is_s):
                src = vg_s[:m, :] if is_s else vbf[:m, tb, :]
                dres = "vg_s" if is_s else ("v", tb)
                S.add("act", lambda e, m=m, src=src, tb=tb: e.activation(out=vtmp[:m, :], in_=src, func=AF.Identity, scale=vstat[:m, tb, 5:6], bias=vstat[:m, tb, 6:7]),
                      reads=[dres, "vs5", "vs6"], writes=["vtmp"], name="vn1")
                S.add("pool", lambda e, m=m: e.tensor_tensor(out=vtmp[:m, :], in0=vtmp[:m, :], in1=lng[:m, l, :], op=ALU.mult),
                      reads=["vtmp", "ln"], writes=["vtmp"], name="vn2")
                if is_s:
                    S.add("pool", lambda e, m=m: e.tensor_tensor(out=vs_f[:m, :], in0=vtmp[:m, :], in1=lnb[:m, l, :], op=ALU.add),
                          reads=["vtmp", "ln"], writes=["vs_f"], name="vn3s")
                    S.add("pool", lambda e, m=m, tb=tb: e.tensor_copy(out=vbf[:m, tb, :], in_=vs_f[:m, :]), reads=["vs_f"], writes=[("v", tb)], name="vn4s")
                    S.add("sp", lambda e: e.dma_start(out=va_s[l], in_=vs_f[:, :]), reads=["vs_f"], dma_sem="o_v", name="va_out")
                else:
                    S.add("pool", lambda e, m=m, tb=tb: e.tensor_tensor(out=vbf[:m, tb, :], in0=vtmp[:m, :], in1=lnb[:m, l, :], op=ALU.add),
                          reads=["vtmp", "ln"], writes=[("v", tb)], name="vn3")
            for blk in blocks:
                v3_pending.append(lambda blk=blk: v3_block(*blk))
            warm(WARM_CONV)
            for j in range(4):
                k, (wc, wx, wb_), sres = consume()
                if tile == 0:
                    S.add("dve", lambda e: e.memset(cinp[:, 0:2], 0.0), writes=["cinp_h"], name="cz")
                else:
                    S.add("dve", lambda e, j=j: e.tensor_copy(out=cinp[:, 0:2], in_=hist_c[:, l, j, :]), reads=[("hist_c", l, j)], writes=["cinp_h"], name="ch")
                for g in groups:
                    c0, n, kind, gid = g[:4]
                    cs = slice(c0, c0 + n)
                    bc, bx, bb = nb(), nb(), nb()
                    pe_mm(bc, banks[bc][:, :n], [(wc[:, kk, :], hT[:, kk, cs]) for kk in range(8)], [sres, ("h", gid)], "cg")
                    pe_mm(bx, banks[bx][:, :n], [(wx[:, kk, :], hT[:, kk, cs]) for kk in range(8)], [sres, ("h", gid)], "xb")
                    pe_mm(bb, banks[bb][:, :n], [(wb_[:, kk, :], hT[:, kk, cs]) for kk in range(8)], [sres, ("h", gid)], "bg")
                    ti = rotate("tmpA", 2)
                    S.add("act", lambda e, ti=ti, bc=bc, n=n: e.activation(out=tmpA[ti][:, :n], in_=banks[bc][:, :n], func=AF.Copy),
                          reads=[("bank", bc)], writes=[("tmpA", ti)], name="ccopy")
                    tb_ = rotate("tmpB", 2)
                    if kind == "p":
                        buf, o = cinp, 2 + c0
                        sh = 1
                        hres = ["cinp_h", ("cinp", g[4] - 1)]
                        wres = [("cinp", g[4])]
                    else:
                        buf, o = cins[:, j, :], 32
                        sh = 16
                        hres = ["cins_h"]
                        wres = [("cins", j)]

                    B_ = ("tmpB", tb_)
                    S.add("dve", lambda e, ti=ti, bx=bx, n=n, buf=buf, o=o: e.tensor_tensor(out=buf[:, o:o + n], in0=tmpA[ti][:, :n], in1=banks[bx][:, :n], op=ALU.mult),
                          reads=[("tmpA", ti), ("bank", bx)], writes=wres, name="cin")
                    S.add("dve", lambda e, tb_=tb_, n=n, buf=buf, o=o, j=j: e.tensor_scalar(out=tmpB[tb_][:, :n], in0=buf[:, o:o + n], scalar1=vcol(l, B_CONV, 2 * 4 + j), scalar2=None, op0=ALU.mult),
                          reads=wres + ["vec"], writes=[B_], name="tap2")
                    S.add("dve", lambda e, tb_=tb_, n=n, buf=buf, o=o, sh=sh, j=j: e.scalar_tensor_tensor(
                        out=tmpB[tb_][:, :n], in0=buf[:, o - sh:o - sh + n], scalar=vcol(l, B_CONV, 1 * 4 + j), in1=tmpB[tb_][:, :n], op0=ALU.mult, op1=ALU.add),
                        reads=wres + hres + [B_, "vec"], writes=[B_], name="tap1")
                    S.add("dve", lambda e, tb_=tb_, n=n, buf=buf, o=o, sh=sh, j=j: e.scalar_tensor_tensor(
                        out=tmpB[tb_][:, :n], in0=buf[:, o - 2 * sh:o - 2 * sh + n], scalar=vcol(l, B_CONV, 0 * 4 + j), in1=tmpB[tb_][:, :n], op0=ALU.mult, op1=ALU.add),
                        reads=wres + hres + [B_, "vec"], writes=[B_], name="tap0")
                    S.add("dve", lambda e, tb_=tb_, bb=bb, n=n, j=j, cs=cs: e.tensor_tensor(out=R[:, 8 + j, cs], in0=tmpB[tb_][:, :n], in1=banks[bb][:, :n], op=ALU.mult),
                          reads=[B_, ("bank", bb)], writes=[("R", 8 + j, gid)], name="ybin")
                    if v3_pending:
                        v3_pending.pop(0)()
                lastg = pgroups[-1]
                S.add("dve", lambda e, j=j: e.tensor_copy(out=hist_c[:, l, j, :], in_=cinp[:, 1024:1026]),
                      reads=[("cinp", lastg[4])], writes=[("hist_c", l, j)], name="chs")
                release(k)
            while v3_pending:
                v3_pending.pop(0)()
            pending_pc = []
            for s2 in range(2):
                k, (wxc,), sres = consume()
                for jj in range(2):
                    j = 2 * s2 + jj
                    w = 2 ** (j + 1)
                    if tile == 0:
                        S.add("dve", lambda e: e.memset(xcb[:, 0:15], 0.0), writes=["xcb_h"], name="pz")
                    else:
                        S.add("dve", lambda e, j=j: e.tensor_copy(out=xcb[:, 0:15], in_=hist_p[:, l, j, :]), reads=[("hist_p", l, j)], writes=["xcb_h"], name="ph")
                    for g in groups:
                        c0, n, kind, gid = g[:4]
                        cs = slice(c0, c0 + n)
                        b = nb()
                        pe_mm(b, banks[b][:, :n], [(wxc[:, kk, jj * 128:(jj + 1) * 128], hT[:, kk, cs]) for kk in range(8)], [sres, ("h", gid)], "xc")
                        if kind == "p":
                            S.add("act", lambda e, b=b, n=n, c0=c0: e.activation(out=xcb[:, 15 + c0:15 + c0 + n], in_=banks[b][:, :n], func=AF.Copy),
                                  reads=[("bank", b)], writes=[("xcb", g[4])], name="xccopy")
                            xin = xcb[:, c0:c0 + n + 15]
                            tot = n + 15
                            sh = 1
                            o = 15
                            rres = ["xcb_h", ("xcb", g[4] - 1), ("xcb", g[4])]
                        else:
                            S.add("act", lambda e, b=b, j=j: e.activation(out=xcs[:, j, 240:304], in_=banks[b][:, :64], func=AF.Copy),
                                  reads=[("bank", b)], writes=[("xcs", j)], name="xccopy_s")
                            xin = xcs[:, j, :]
                            tot = 304
                            sh = 16
                            o = 240
                            rres = [("xcs_h", 0), ("xcs_h", 120), ("xcs", j)]
                        pi = rotate("pooled", 4)
                        first = (tile == 0 and kind == "p" and c0 == 0)

                        cur, cres = xin, list(rres)
                        for m_ in range(1, j + 2):
                            kk = (2 ** (m_ - 1)) * sh
                            lo = (2 ** m_ - 1) * sh
                            li = (m_ - 1) % 2
                            dst = lev[li][:, 0:tot]
                            S.add("dve", lambda e, dst=dst, cur=cur, lo=lo, kk=kk, tot=tot: e.tensor_tensor(out=dst[:, lo:tot], in0=cur[:, lo:tot], in1=cur[:, lo - kk:tot - kk], op=ALU.add),
                                  reads=cres, writes=[("lev", li)], name="plev")
                            cur, cres = dst, [("lev", li)]
                        if first:
                            lo_i = (j + 1) % 2
                            tbf = lev[lo_i]
                            S.add("dve", lambda e, n=n, cur=cur, o=o, w=w, xin=xin, pi=pi: e.scalar_tensor_tensor(
                                out=pooled[pi][:, 16:n], in0=cur[:, o + 16:o + n], scalar=1.0 / w, in1=xin[:, o + 16:o + n], op0=ALU.mult, op1=ALU.subtract),
                                reads=cres + rres, writes=[("pooled", pi)], name="pfin")
                            S.add("dve", lambda e, tbf=tbf, cur=cur, o=o, j=j: e.tensor_tensor(out=tbf[:, 0:16], in0=cur[:, o:o + 16], in1=invcnt[:, j, :], op=ALU.mult),
                                  reads=cres + ["invcnt"], writes=[("lev", lo_i)], name="pfix1")
                            S.add("dve", lambda e, tbf=tbf, xin=xin, o=o, pi=pi: e.tensor_tensor(out=pooled[pi][:, 0:16], in0=tbf[:, 0:16], in1=xin[:, o:o + 16], op=ALU.subtract),
                                  reads=[("lev", lo_i)] + rres, writes=[("pooled16", pi)], name="pfix2")
                            pres_ = [("pooled", pi), ("pooled16", pi)]
                        else:
                            S.add("dve", lambda e, n=n, cur=cur, o=o, w=w, xin=xin, pi=pi: e.scalar_tensor_tensor(
                                out=pooled[pi][:, :n], in0=cur[:, o:o + n], scalar=1.0 / w, in1=xin[:, o:o + n], op0=ALU.mult, op1=ALU.subtract),
                                reads=cres + rres, writes=[("pooled", pi), ("pooled16", pi)], name="pfin")
                            pres_ = [("pooled", pi), ("pooled16", pi)]
                        def do_pc(n=n, j=j, cs=cs, pi=pi, pres_=pres_, gid=gid):
                            b2 = nb()
                            pe_mm(b2, banks[b2][:, :n], [(cw_b[:, l, j, :], pooled[pi][:, :n])], pres_ + ["cw"], "pc")
                            S.add("act", lambda e, b2=b2, n=n, j=j, cs=cs: e.activation(out=R[:, 12 + j, cs], in_=banks[b2][:, :n], func=AF.Copy, scale=vcol(l, C_SCALE, j)),
                                  reads=[("bank", b2), "vec"], writes=[("R", 12 + j, gid)], name="pcs")
                        pending_pc.append(do_pc)
                        if len(pending_pc) > 3:
                            pending_pc.pop(0)()
                    lastg = pgroups[-1]
                    S.add("dve", lambda e, j=j: e.tensor_copy(out=hist_p[:, l, j, :], in_=xcb[:, 1024:1039]),
                          reads=[("xcb", lastg[4])], writes=[("hist_p", l, j)], name="phs")
                release(k)
            while pending_pc:
                pending_pc.pop(0)()
            if tile == 1:
                b, b2_ = nb(), nb()

                def trc(e, b=b, b2_=b2_):
                    last = None
                    for j in range(4):
                        e.transpose(banks[b][0:2, j * 128:(j + 1) * 128], hist_c[:, l, j, :], ident[:, :])
                    for j in range(4):
                        last = e.transpose(banks[b2_][0:32, j * 128:(j + 1) * 128], cins[:, j, 64:96], ident[:, :])
                    return last
                S.add("pe", trc, reads=[("hist_c", l, j) for j in range(4)] + [("cins", j) for j in range(4)] + ["ident"],
                      writes=[("bank", b), ("bank", b2_)], name="trc")

                def cpc(e, b=b, b2_=b2_):
                    e.activation(out=stage[1][0:2, 0:512], in_=banks[b][0:2, :], func=AF.Copy)
                    return e.activation(out=stage[1][0:32, 512:1024], in_=banks[b2_][0:32, :], func=AF.Copy)
                S.add("act", cpc, reads=[("bank", b), ("bank", b2_)], writes=[("stage", 1), ("stageb", 1)], name="cpc")
                S.add("sp", lambda e: [e.dma_start(out=conv_p[l], in_=stage[1][0:2, 0:512]), e.dma_start(out=conv_s[l], in_=stage[1][0:32, 512:1024])],
                      reads=[("stage", 1), ("stageb", 1)], dma_sem="o_small", ndma=2, name="conv_out")
            if tile == 1:
                b, b2_ = nb(), nb()

                def trp(e, b=b, b2_=b2_):
                    last = None
                    for j in range(4):
                        e.transpose(banks[b][0:15, j * 128:(j + 1) * 128], hist_p[:, l, j, :], ident[:, :])
                    for j in range(4):
                        last = e.transpose(banks[b2_][0:64, j * 128:(j + 1) * 128], xcs[:, j, 240:304], ident[:, :])
                    return last
                S.add("pe", trp, reads=[("hist_p", l, j) for j in range(4)] + [("xcs", j) for j in range(4)] + ["ident"],
                      writes=[("bank", b), ("bank", b2_)], name="trp")

                def cpp(e, b=b, b2_=b2_):
                    e.activation(out=stage[1][0:15, 0:512], in_=banks[b][0:15, :], func=AF.Copy)
                    return e.activation(out=stage[1][0:64, 512:1024], in_=banks[b2_][0:64, :], func=AF.Copy)
                S.add("act", cpp, reads=[("bank", b), ("bank", b2_)], writes=[("stage", 1), ("stageb", 1)], name="cpp")
                S.add("sp", lambda e: [e.dma_start(out=pool_p[l], in_=stage[1][0:15, 0:512]), e.dma_start(out=pool_s[l, 176:240, :], in_=stage[1][0:64, 512:1024])],
                      reads=[("stage", 1), ("stageb", 1)], dma_sem="o_small", ndma=2, name="pool_out")
            for g in groups:
                c0, n, kind, gid = g[:4]
                cs = slice(c0, c0 + n)
                for j in range(4):
                    b = nb()
                    if kind == "p":
                        grp = [(banks[b][:, i * 128:(i + 1) * 128], [(vbf[:, c0 // 128 + i, j * 128:(j + 1) * 128], wsT_b[:, l * 4 + j, :])]) for i in range(n // 128)]
                        pe_multi(b, grp, [("v", c0 // 128 + i) for i in range(n // 128)] + ["wsT_b"], "sgate")
                        bias_ap = bias4[:, l, j * 128:(j + 1) * 128]
                    else:
                        pe_mm(b, banks[b][:, :64], [(vbf[:64, 8, j * 128:(j + 1) * 128], bd_b[:, l * 4 + j, :])], [("v", 8), "bd_b"], "sgate_s")
                        bias_ap = biass[:, l, j * 64:(j + 1) * 64]
                    tb_ = rotate("tmpB", 2)

                    if kind == "p":
                        S.add("dve", lambda e, b=b, n=n, tb_=tb_, bias_ap=bias_ap: e.tensor_tensor(
                            out=tmpB[tb_][:, :n].rearrange("p (r t) -> p r t", t=128), in0=banks[b][:, :n].rearrange("p (r t) -> p r t", t=128),
                            in1=bias_ap.unsqueeze(1).to_broadcast([128, n // 128, 128]), op=ALU.add),
                            reads=[("bank", b), "bias"], writes=[("tmpB", tb_)], name="sgb")
                    else:
                        S.add("dve", lambda e, b=b, n=n, tb_=tb_, bias_ap=bias_ap: e.tensor_tensor(out=tmpB[tb_][:, :n], in0=banks[b][:, :n], in1=bias_ap, op=ALU.add),
                              reads=[("bank", b), "bias"], writes=[("tmpB", tb_)], name="sgb")
                    S.add("dve", lambda e, n=n, tb_=tb_, j=j, cs=cs: e.tensor_tensor(out=R[:, 4 + j, cs], in0=tmpB[tb_][:, :n], in1=R[:, j, cs], op=ALU.mult),
                          reads=[("tmpB", tb_), ("R", j, gid)], writes=[("R", 4 + j, gid)], name="sgmul")
            for dp in range(4):
                ko, outs_w, ores = consume()
                for dd in range(2):
                    d = 2 * dp + dd
                    kg, gates_w, gres = consume()
                    for g in mgroups:
                        c0, n, kind, gid = g[:4]
                        cs = slice(c0, c0 + n)
                        acc_i = rotate("tmpB", 2)
                        for br in range(3):
                            bgt, by = nb(), nb()
                            pe_mm(bgt, banks[bgt][:, :n], [(gates_w[br][:, kk, :], hT[:, kk, cs]) for kk in range(8)], [gres, ("h", gid)], "gate")
                            pe_mm(by, banks[by][:, :n], [(outs_w[br][:, kk, dd * 128:(dd + 1) * 128], R[:, 4 + 4 * br + kk, cs]) for kk in range(4)],
                                  [ores] + [("R", 4 + 4 * br + kk, gid) for kk in range(4)], "ybr")
                            ti = rotate("tmpA", 2)
                            S.add("act", lambda e, ti=ti, bgt=bgt, n=n: e.activation(out=tmpA[ti][:, :n], in_=banks[bgt][:, :n], func=AF.Sigmoid),
                                  reads=[("bank", bgt)], writes=[("tmpA", ti)], name="gsig")

                            A_, ACC = ("tmpA", ti), ("tmpB", acc_i)
                            if br == 0:
                                S.add("dve", lambda e, ti=ti, by=by, n=n, acc_i=acc_i: e.tensor_tensor(out=tmpB[acc_i][:, :n], in0=tmpA[ti][:, :n], in1=banks[by][:, :n], op=ALU.mult),
                                      reads=[A_, ("bank", by)], writes=[ACC], name="mg0")
                            else:
                                S.add("dve", lambda e, ti=ti, by=by, n=n: e.tensor_tensor(out=tmpA[ti][:, :n], in0=tmpA[ti][:, :n], in1=banks[by][:, :n], op=ALU.mult),
                                      reads=[A_, ("bank", by)], writes=[A_], name="mgm")
                                if br == 1:
                                    S.add("dve", lambda e, ti=ti, n=n, acc_i=acc_i: e.tensor_tensor(out=tmpB[acc_i][:, :n], in0=tmpB[acc_i][:, :n], in1=tmpA[ti][:, :n], op=ALU.add),
                                          reads=[A_, ACC], writes=[ACC], name="mga")
                                else:
                                    S.add("dve", lambda e, ti=ti, n=n, acc_i=acc_i, d=d, cs=cs: e.tensor_tensor(out=R[:, MERGED[d], cs], in0=tmpB[acc_i][:, :n], in1=tmpA[ti][:, :n], op=ALU.add),
                                          reads=[A_, ACC], writes=[("R", MERGED[d], gid)], name="mgf")
                    release(kg)
                release(ko)
            for s4 in range(4):
                k, (wo,), sres = consume()
                for dd in range(2):
                    d = 2 * s4 + dd
                    for g in mgroups:
                        c0, n, _, gid = g[:4]
                        cs = slice(c0, c0 + n)
                        b = nb()
                        pe_mm(b, banks[b][:, :n], [(wo[:, kk, dd * 128:(dd + 1) * 128], R[:, MERGED[kk], cs]) for kk in range(8)],
                              [sres] + [("R", MERGED[kk], gid) for kk in range(8)], "wo")
                        S.add("dve", lambda e, b=b, n=n, d=d, cs=cs: e.tensor_tensor(out=xT[:, d, cs], in0=xT[:, d, cs], in1=banks[b][:, :n], op=ALU.add),
                              reads=[("bank", b), ("x", d, gid)], writes=[("x", d, gid)], name="wores")
                        post_x_update(d, g, "h")
                        maybe_early_norm(d, g)
                release(k)
            end_phase()

        def load_x(tile, groups):
            blocks = [(xp[tile * 1024 + i * 128: tile * 1024 + (i + 1) * 128, :], i * 128, 128, i // 4) for i in range(8)]
            if tile == 1:
                blocks.append((xs[:, :], 1024, 64, 2))
            for (src, t0, m, gi) in blocks:
                gid = gran(t0, m)
                si = rotate("stage", 2)
                S.add("sp", lambda e, si=si, src=src, m=m: e.dma_start(out=stage[si][:m, :], in_=src), writes=[("stage", si)], dma_sem=f"sin{si}", name="xload")
                warm(WARM_XLD)
                for hh in range(2):
                    b = nb()

                    def tr(e, si=si, b=b, m=m, hh=hh):
                        last = None
                        for c in range(4):
                            last = e.transpose(banks[b][:, c * 128:c * 128 + m], stage[si][:m, (4 * hh + c) * 128:(4 * hh + c + 1) * 128], ident[:m, :m])
                        return last
                    S.add("pe", tr, reads=[("stage", si), "ident"], writes=[("bank", b)], name="xtr")
                    src_v = banks[b][:, :].rearrange("p (c t) -> p c t", t=128)[:, :, :m]
                    if hh == 0:
                        S.add("act", lambda e, src_v=src_v, t0=t0, m=m, hh=hh: e.activation(out=xT[:, 4 * hh:4 * hh + 4, t0:t0 + m], in_=src_v, func=AF.Copy),
                              reads=[("bank", b)], writes=[("x", d, gid) for d in range(4)], name="xcp")
                    else:
                        S.add("dve", lambda e, src_v=src_v, t0=t0, m=m, hh=hh: e.tensor_copy(out=xT[:, 4 * hh:4 * hh + 4, t0:t0 + m], in_=src_v),
                              reads=[("bank", b)], writes=[("x", d, gid) for d in range(4, 8)], name="xcp")

        def store_y(tile, groups):
            OFF = L * NV
            if not norm_done[0]:
                for g in groups:
                    rmsnorm(g, lambda c: vec_sb[:, OFF + c:OFF + c + 1], final=True, fused=True)
            warm(WARM_FFN)
            blocks = [(yp[tile * 1024 + i * 128: tile * 1024 + (i + 1) * 128, :], i * 128, 128, i // 4) for i in range(8)]
            if tile == 1:
                blocks.append((ys[:, :], 1024, 64, 2))
            for (dst, t0, m, gi) in blocks:
                g = (t0, m, None, gran(t0, m))
                si = rotate("stage", 2)
                for hh in range(2):
                    b = nb()

                    def tr(e, b=b, m=m, hh=hh, t0=t0):
                        last = None
                        for c in range(4):
                            last = e.transpose(banks[b][:m, c * 128:(c + 1) * 128], xT[:, 4 * hh + c, t0:t0 + m], ident[:, :])
                        return last
                    S.add("pe", tr, reads=xres(g) + ["ident"], writes=[("bank", b)], name="ytr")
                    if hh == 0:
                        S.add("act", lambda e, b=b, m=m, si=si: e.activation(out=stage[si][:m, 0:512], in_=banks[b][:m, :], func=AF.Copy),
                              reads=[("bank", b)], writes=[("stage", si)], name="ycp")
                    else:
                        S.add("dve", lambda e, b=b, m=m, si=si: e.tensor_copy(out=stage[si][:m, 512:1024], in_=banks[b][:m, :]),
                              reads=[("bank", b)], writes=[("stageb", si)], name="ycp")
                S.add("sp", lambda e, si=si, dst=dst, m=m: e.dma_start(out=dst, in_=stage[si][:m, :]), reads=[("stage", si), ("stageb", si)],
                      writes=[("stage", si)], dma_sem=f"sout{si}", name="ystore")

        groups_now = []
        for tile in range(2):
            groups = [(0, 512, "p", gran(0, 512), 0), (512, 512, "p", gran(512, 512), 1)]
            mgroups = list(groups)
            if tile == 1:
                groups.append((1024, 64, "s", gran(1024, 64), 2))
                mgroups = [(0, 384, "m", gran(0, 384), 0), (384, 384, "m", gran(384, 384), 1), (768, 320, "m", gran(768, 320), 2)]
            groups_now[:] = mgroups
            stat_bank.clear()
            for gi in range(len(mgroups)):
                stat_bank[gi] = 7 - gi
            free_banks[0] = list(range(8 - len(mgroups)))
            load_x(tile, groups)

            def dumpx(i, groups=groups, tile=tile):
                if KDEBUG and tile == 0:
                    S.add("sp", lambda e: e.dma_start(out=dbg_xs[i], in_=xT[:, :, 0:64]), reads=xres(groups[0]), dma_sem=f"dx{i}", name="dumpx")
            OFFF = L * NV

            def mk(fn_col, final=False):
                return lambda g: rmsnorm(g, fn_col, final=final, fused=True)
            norm_done[0] = False
            for l in range(L):
                if EARLY_NORM:
                    early_norm[0] = mk(lambda c, l=l: vcol(l, G_MIX, c))
                ffn(l, 0, mgroups, interleave=load_p(l, tile, groups), fused_in=(l > 0))
                norm_done[0] = EARLY_NORM
                dumpx(4 * l + 0)
                if EARLY_NORM:
                    early_norm[0] = mk(lambda c, l=l: vcol(l, G_FFN2, c))
                mixer(l, tile, groups, mgroups)
                dumpx(4 * l + 1)
                if EARLY_NORM:
                    early_norm[0] = mk(lambda c, l=l: vcol(l, G_PLE, c))
                ffn(l, 1, mgroups, fused_in=True)
                dumpx(4 * l + 2)
                if EARLY_NORM:
                    if l + 1 < L:
                        early_norm[0] = mk(lambda c, l=l: vcol(l + 1, G_FFN1, c))
                    else:
                        early_norm[0] = mk(lambda c: vec_sb[:, OFFF + c:OFFF + c + 1], final=True)
                ple(l, tile, mgroups)
                dumpx(4 * l + 3)
            early_norm[0] = None
            store_y(tile, mgroups)
            norm_done[0] = False
        assert slab_state["next_use"] == len(slabs), (slab_state, len(slabs))

        S.finalize()
        run = S.runner(sems, dsems)
        with nc.Block() as block:
            @block.tensor
            def _(e):
                run("pe", e)

            @block.scalar
            def _(e):
                run("act", e)

            @block.vector
            def _(e):
                run("dve", e)

            @block.gpsimd
            def _(e):
                run("pool", e)

            @block.sync
            def _(e):
                run("sp", e)
                for n_, c_ in S.dma_counts.items():
                    e.wait_ge(dsems[n_], c_)
    return nc


def _prep_inputs(inp):
    f = lambda a: np.ascontiguousarray(np.asarray(a, dtype=np.float32))
    a_ws = f(inp["a_ws"])
    a_bs = f(inp["a_bs"])
    shared = {}
    for nme in ("w_ffn1_up", "w_ffn2_up", "w_ffn1_down", "w_ffn2_down", "w_in", "a_out", "b_out", "c_out", "w_o",
                "w_ple_gate", "w_ple_proj", "c_w", "a_ln_g", "a_ln_b"):
        shared[nme] = f(inp[nme])
    shared["wsT"] = f(a_ws.transpose(0, 1, 3, 2))
    bd = np.zeros((L, 4, 64, 64), np.float32)
    for q in range(16):
        for s in range(4):
            for t in range(4):
                bd[:, :, s * 16 + q, t * 16 + q] = a_ws[:, :, t, s]
    shared["bd"] = bd
    shared["bias4"] = f(a_bs.reshape(L, 4 * 128))
    shared["biass"] = f(np.repeat(a_bs[:, :, 0:4], 16, axis=2).reshape(L, 4 * 64))
    vec = np.zeros((128, L * NV + 8), np.float32)
    col = lambda v, nchunk: f(v).reshape(nchunk, 128).T
    for l in range(L):
        o = l * NV
        vec[:, o + 0:o + 8] = col(inp["g_ffn1"][l], 8)
        vec[:, o + 8:o + 16] = col(inp["g_mix"][l], 8)
        vec[:, o + 16:o + 24] = col(inp["g_ffn2"][l], 8)
        vec[:, o + 24:o + 32] = col(inp["g_ple"][l], 8)
        vec[:, o + 32:o + 36] = col(inp["c_scale"][l], 4)
        for k in range(3):
            vec[:, o + 36 + 4 * k:o + 36 + 4 * k + 4] = col(inp["b_conv"][l, k], 4)
    vec[:, L * NV:L * NV + 8] = col(inp["g_final"], 8)
    shared["vecs"] = vec
    xpr, xsa = f(inp["x_prompt"]), f(inp["x_sample"])
    sc, spl = f(inp["state_conv"]), f(inp["state_pool"])
    ppr, psa = f(inp["p_prompt"]), f(inp["p_sample"])
    maps = []
    for c in range(8):
        sl = slice(16 * c, 16 * c + 16)
        m = dict(shared)
        m["xp"] = xpr[c]
        m["xs"] = f(xsa[sl].transpose(1, 0, 2).reshape(64, D))
        m["sconv"] = f(sc[:, sl].transpose(0, 2, 1, 3).reshape(L, 32, 512))
        m["spool"] = f(spl[:, sl].transpose(0, 2, 1, 3).reshape(L, 240, 512))
        m["pp"] = f(ppr[:, c])
        m["ps"] = f(psa[:, sl].transpose(0, 2, 1, 3).reshape(L, 64, 256))
        maps.append(m)
    return maps


_NC_CACHE = {}


def kernel(**inputs):
    in_maps = _prep_inputs(inputs)
    if "nc" not in _NC_CACHE:
        _NC_CACHE["nc"] = build_nc()
    nc = _NC_CACHE["nc"]
    res = run_bass_kernel_spmd(nc, in_maps, core_ids=list(range(8)))
    r = res.results
    y_prompt = np.stack([r[c]["yp"] for c in range(8)]).astype(np.float32)
    y_sample = np.concatenate([r[c]["ys"].reshape(4, 16, D).transpose(1, 0, 2) for c in range(8)], axis=0).astype(np.float32)
    conv_prompt = np.stack([r[c]["conv_p"] for c in range(8)], axis=1).astype(np.float32)
    conv_sample = np.concatenate([r[c]["conv_s"].reshape(L, 2, 16, 512).transpose(0, 2, 1, 3) for c in range(8)], axis=1).astype(np.float32)
    pool_prompt = np.stack([r[c]["pool_p"] for c in range(8)], axis=1).astype(np.float32)
    pool_sample = np.concatenate([r[c]["pool_s"].reshape(L, 15, 16, 512).transpose(0, 2, 1, 3) for c in range(8)], axis=1).astype(np.float32)
    va = np.concatenate([r[c]["va_s"].reshape(L, 4, 16, 512).transpose(0, 2, 1, 3) for c in range(8)], axis=1).astype(np.float32)
    return (np.ascontiguousarray(y_prompt), np.ascontiguousarray(y_sample), np.ascontiguousarray(conv_prompt),
            np.ascontiguousarray(conv_sample), np.ascontiguousarray(pool_prompt), np.ascontiguousarray(pool_sample),
            np.ascontiguousarray(va))
```

```python
import numpy as np
import os
KDEBUG = os.environ.get('KDEBUG', '') == '1'
from contextlib import ExitStack
import concourse.bass as bass
import concourse.mybir as mybir
from concourse.bass_utils import run_bass_kernel_spmd

F32 = mybir.dt.float32
BF16 = mybir.dt.bfloat16
AF = mybir.ActivationFunctionType
ALU = mybir.AluOpType

L = 2
D = 1024
DFF = 2816
NF = DFF // 128
DIN = 6144
EPS = 1e-6
NSLOT = 4
SLOT_EL = 3072
ENGINES = ("pe", "act", "dve", "pool", "sp")
SELF_SYNC = ("act", "dve", "pool")
NV = 48
WARM_FFN, WARM_MIX, WARM_PLE = 0, 0, 0
WARM_CONV, WARM_NSTAT, WARM_XLD = 0, 30, 15
EARLY_NORM = True
WARM_END = 35


class Op:
    __slots__ = ("eng", "emit", "reads", "writes", "dma_sem", "deps", "signals", "count", "idx", "ndma", "name")


class Gran(tuple):
    pass


def gran(c0, n):
    return Gran(range(c0 // 64, (c0 + n + 63) // 64))


def expand_res(rs):
    rs2 = []
    for r in rs:
        if isinstance(r, tuple) and len(r) == 2 and r[0] == "h":
            rs2.extend(("h", r[1], c) for c in range(8))
        else:
            rs2.append(r)
    out = []
    for r in rs2:
        if isinstance(r, tuple) and any(isinstance(x, Gran) for x in r):
            i = [isinstance(x, Gran) for x in r].index(True)
            for gnum in r[i]:
                out.append(r[:i] + (gnum,) + r[i + 1:])
        else:
            out.append(r)
    return out


class Sched:
    def __init__(self):
        self.ops = []
        self.last_writer = {}
        self.readers = {}
        self.dma_counts = {}

    def add(self, eng, emit, reads=(), writes=(), dma_sem=None, ndma=1, name=""):
        op = Op()
        op.eng, op.emit, op.reads, op.writes = eng, emit, tuple(expand_res(reads)), tuple(expand_res(writes))
        op.dma_sem, op.ndma, op.name = dma_sem, ndma, name
        op.signals, op.count = False, None
        op.idx = len(self.ops)
        deps = set()
        for r in op.reads:
            w = self.last_writer.get(r)
            if w is not None:
                deps.add(w)
        for w_ in op.writes:
            w = self.last_writer.get(w_)
            if w is not None:
                deps.add(w)
            deps.update(self.readers.get(w_, ()))
        deps.discard(op.idx)
        op.deps = sorted(deps)
        for r in op.reads:
            self.readers.setdefault(r, []).append(op.idx)
        for w_ in op.writes:
            self.last_writer[w_] = op.idx
            self.readers[w_] = []
        self.ops.append(op)
        return op

    def finalize(self):
        ops = self.ops
        for op in ops:
            for d in op.deps:
                dop = ops[d]
                if dop.dma_sem is None and (dop.eng != op.eng or dop.eng in SELF_SYNC):
                    dop.signals = True
        cnt = {e: 0 for e in ENGINES}
        for op in ops:
            if op.dma_sem is not None:
                c = self.dma_counts.get(op.dma_sem, 0) + 16 * op.ndma
                self.dma_counts[op.dma_sem] = c
                op.count = c
            elif op.signals:
                cnt[op.eng] += 1
                op.count = cnt[op.eng]

    def runner(self, sems, dma_sems):
        ops = self.ops
        per = {e: [] for e in ENGINES}
        for op in ops:
            per[op.eng].append(op)

        def run(ename, e):
            waited = {}
            for op in per[ename]:
                need = {}
                for d in op.deps:
                    dop = ops[d]
                    if dop.dma_sem is not None:
                        key = ("dma", dop.dma_sem)
                    elif dop.eng == ename and ename not in SELF_SYNC:
                        continue
                    else:
                        key = ("eng", dop.eng)
                    if need.get(key, 0) < dop.count:
                        need[key] = dop.count
                for key, c in need.items():
                    if waited.get(key, 0) >= c:
                        continue
                    waited[key] = c
                    e.wait_ge(dma_sems[key[1]] if key[0] == "dma" else sems[key[1]], c)
                ins = op.emit(e)
                if op.dma_sem is not None:
                    lst = ins if isinstance(ins, (list, tuple)) else [ins]
                    assert len(lst) == op.ndma, (op.name, len(lst), op.ndma)
                    for i_ in lst:
                        i_.then_inc(dma_sems[op.dma_sem], 16)
                elif op.signals:
                    ins.then_inc(sems[ename], 1)
        return run


def build_nc():
    nc = bass.Bass("TRN2", target_bir_lowering=False)
    S = Sched()

    def din(name, shape):
        return nc.dram_tensor(name, list(shape), F32, kind="ExternalInput").ap()

    def dout(name, shape):
        return nc.dram_tensor(name, list(shape), F32, kind="ExternalOutput").ap()

    xp = din("xp", [2048, D]); xs = din("xs", [64, D])
    sconv = din("sconv", [L, 32, 512]); spool = din("spool", [L, 240, 512])
    pp = din("pp", [L, 2048, 256]); ps_ = din("ps", [L, 64, 256])
    vecs = din("vecs", [128, L * NV + 8])
    w_up = [din("w_ffn1_up", [L, D, 2 * DFF]), din("w_ffn2_up", [L, D, 2 * DFF])]
    w_dn = [din("w_ffn1_down", [L, DFF, D]), din("w_ffn2_down", [L, DFF, D])]
    w_in = din("w_in", [L, D, DIN])
    a_out = din("a_out", [L, 512, D]); b_out = din("b_out", [L, 512, D]); c_out = din("c_out", [L, 512, D])
    w_o = din("w_o", [L, D, D]); w_pg = din("w_ple_gate", [L, D, D]); w_pp = din("w_ple_proj", [L, 256, D])
    c_w = din("c_w", [L, 4, 128, 128])
    wsT_d = din("wsT", [L, 4, 128, 128]); bd_d = din("bd", [L, 4, 64, 64])
    bias4_d = din("bias4", [L, 4 * 128]); biass_d = din("biass", [L, 4 * 64])
    lng_d = din("a_ln_g", [L, 512]); lnb_d = din("a_ln_b", [L, 512])

    yp = dout("yp", [2048, D]); ys = dout("ys", [64, D])
    conv_p = dout("conv_p", [L, 2, 512]); conv_s = dout("conv_s", [L, 32, 512])
    pool_p = dout("pool_p", [L, 15, 512]); pool_s = dout("pool_s", [L, 240, 512])
    va_s = dout("va_s", [L, 64, 512])
    if KDEBUG:
        dbg_x = dout("dbg_x", [128, 8, 64]); dbg_h = nc.dram_tensor("dbg_h", [128, 8, 64], BF16, kind="ExternalOutput").ap()
        dbg_r = nc.dram_tensor("dbg_r", [128, 22, 64], BF16, kind="ExternalOutput").ap()
        dbg_r0 = nc.dram_tensor("dbg_r0", [128, 22, 64], BF16, kind="ExternalOutput").ap()
        dbg_xs = dout("dbg_xs", [8, 128, 8, 64])

    TT = 1088
    with ExitStack() as es:
        def sb(name, shape, dt=F32):
            return es.enter_context(nc.sbuf_tensor("sb_" + name, list(shape), dt))

        xT = sb("xT", [128, 8, TT])
        hT = sb("hT", [128, 8, TT], BF16)
        R = sb("R", [128, 22, TT], BF16)
        vbf = sb("vbf", [128, 9, 512], BF16)
        pT = sb("pT", [128, 2, TT], BF16)
        slots = [sb(f"slot{i}", [128, SLOT_EL], BF16) for i in range(NSLOT)]
        stage = [sb(f"stage{i}", [128, D]) for i in range(2)]
        pstage = [sb(f"pstage{i}", [128, 256]) for i in range(2)]
        tmpA = [sb(f"tmpA{i}", [128, 512]) for i in range(2)]
        tmpB = [sb(f"tmpB{i}", [128, 512]) for i in range(2)]
        msb = [sb(f"ms{i}", [128, 512]) for i in range(1)]
        rstd = [sb(f"rstd{i}", [128, 512]) for i in range(1)]
        cinp = sb("cinp", [128, 2 + 1024])
        cins = sb("cins", [128, 4, 96])
        xcb = sb("xcb", [128, 15 + 1024])
        xcs = sb("xcs", [128, 4, 304])
        lev = [sb(f"lev{i}", [128, 15 + 512]) for i in range(2)]
        pooled = [sb(f"pooled{i}", [128, 512], BF16) for i in range(4)]
        hist_c = sb("hist_c", [128, L, 4, 2])
        hist_p = sb("hist_p", [128, L, 4, 15])
        vs_f = sb("vs_f", [64, 512])
        vstat = sb("vstat", [128, 9, 8])
        vg_s = sb("vg_s", [64, 512])
        vtmp = sb("vtmp", [128, 512])
        ident = sb("ident", [128, 128]); mask = sb("mask", [128, 128])
        ones_bf = sb("ones_bf", [128, 128], BF16)
        nhalf = sb("nhalf", [128, 1]); epsb = sb("epsb", [128, 1])
        invcnt = sb("invcnt", [128, 4, 16])
        vec_sb = sb("vec_sb", [128, L * NV + 8])
        wsT_b = sb("wsT_b", [128, L * 4, 128], BF16)
        bd_b = sb("bd_b", [64, L * 4, 64], BF16)
        cw_b = sb("cw_b", [128, L, 4, 128], BF16)
        bias4 = sb("bias4", [128, L, 4 * 128]); biass = sb("biass", [128, L, 4 * 64])
        lng = sb("lng", [128, L, 512]); lnb = sb("lnb", [128, L, 512])

        banks = [es.enter_context(nc.psum_tensor(f"bank{i}", [128, 512], F32)) for i in range(8)]
        sems = {e: es.enter_context(nc.semaphore("s_" + e)) for e in ENGINES}
        dnames = [f"w{i}" for i in range(NSLOT)] + ["sin0", "sin1", "sout0", "sout1", "pin0", "pin1", "const",
                                                    "cw", "state", "o_small", "o_v", "d2d", "dbg", "dbg2", "dbg3"] + [f"dx{i}" for i in range(8)]
        dsems = {n: es.enter_context(nc.semaphore("d_" + n)) for n in dnames}

        bank_ctr = [0]

        free_banks = [list(range(8))]

        def nb():
            fb = free_banks[0]
            b = fb[bank_ctr[0] % len(fb)]
            bank_ctr[0] += 1
            return b

        rot = {}

        def rotate(key, n):
            i = rot.get(key, 0)
            rot[key] = i + 1
            return i % n

        ones_f = tmpB[0][:, 0:128]
        S.add("pool", lambda e: e.memset(ones_f, 1.0), writes=[("tmpB", 0)], name="c_ones")
        S.add("pool", lambda e: e.affine_select(out=ident[:], in_=ones_f, pattern=[[1, 128]], compare_op=ALU.is_equal,
                                               fill=0.0, base=0, channel_multiplier=-1), reads=[("tmpB", 0)], writes=["ident"], name="c_ident")
        S.add("pool", lambda e: e.affine_select(out=mask[:], in_=ones_f, pattern=[[1, 128]], compare_op=ALU.is_ge,
                                               fill=0.0, base=0, channel_multiplier=-1), reads=[("tmpB", 0)], writes=["mask"], name="c_mask")

        def const_setup(e):
            e.memset(ones_bf[:], 1.0 / 1024.0)
            e.memset(nhalf[:], -0.5)
            e.memset(epsb[:], EPS)
            for j in range(4):
                w = 2 ** (j + 1)
                e.memset(invcnt[:, j, w - 1:16], 1.0 / w)
                for t in range(w - 1):
                    e.memset(invcnt[:, j, t:t + 1], 1.0 / (t + 1))
            e.memset(vstat[:], 1.0)
            return e.memset(hist_c[:], 0.0)
        S.add("pool", const_setup, writes=["ones_bf", "nhalf", "epsb", "invcnt", "hist", "vstat01"], name="consts")

        wsT_f = stage[0][:, :].rearrange("p (g t) -> p g t", t=128)
        bd_f = stage[1][0:64, 0:512].rearrange("p (g t) -> p g t", t=64)

        def const_loads(e):
            r = [e.dma_start(out=vec_sb[:], in_=vecs)]
            r.append(e.dma_start(out=wsT_f, in_=wsT_d.rearrange("l g s t -> s (l g) t")))
            r.append(e.dma_start(out=bd_f, in_=bd_d.rearrange("l g s t -> s (l g) t")))
            for l in range(L):
                r.append(e.dma_start(out=bias4[:, l, :], in_=bias4_d[l].partition_broadcast(128)))
                r.append(e.dma_start(out=biass[:, l, :], in_=biass_d[l].partition_broadcast(128)))
                r.append(e.dma_start(out=lng[:, l, :], in_=lng_d[l].partition_broadcast(128)))
                r.append(e.dma_start(out=lnb[:, l, :], in_=lnb_d[l].partition_broadcast(128)))
            return r
        S.add("sp", const_loads, writes=["vec", ("stage", 0), ("stage", 1), "bias", "ln"], dma_sem="const", ndma=3 + 4 * L, name="const_loads")
        S.add("pool", lambda e: e.dma_start(out=cw_b[:], in_=c_w.rearrange("l g c d -> c l g d")), writes=["cw"], dma_sem="cw", name="cw")

        S.add("dve", lambda e: e.tensor_tensor(out=wsT_f, in0=wsT_f, in1=mask[:, :].unsqueeze(1).to_broadcast([128, L * 4, 128]), op=ALU.mult),
              reads=[("stage", 0), "mask"], writes=[("stage", 0)], name="ws_mask")
        S.add("dve", lambda e: e.tensor_copy(out=wsT_b[:], in_=wsT_f), reads=[("stage", 0)], writes=["wsT_b"], name="ws_cast")
        S.add("dve", lambda e: e.tensor_tensor(out=bd_f, in0=bd_f, in1=mask[0:64, 0:64].unsqueeze(1).to_broadcast([64, L * 4, 64]), op=ALU.mult),
              reads=[("stage", 1), "mask"], writes=[("stage", 1)], name="bd_mask")
        S.add("dve", lambda e: e.tensor_copy(out=bd_b[:], in_=bd_f), reads=[("stage", 1)], writes=["bd_b"], name="bd_cast")

        def vcol(l, off, c):
            o = l * NV + off + c
            return vec_sb[:, o:o + 1]
        G_FFN1, G_MIX, G_FFN2, G_PLE, C_SCALE, B_CONV = 0, 8, 16, 24, 32, 36

        slabs = []
        slab_state = {"next_load": 0, "next_use": 0}

        def w_part(ap2d, kc, c):
            return (ap2d, kc, c)

        def plan_layer(l):
            def ffn(which):
                for f in range(NF):
                    slabs.append([w_part(w_up[which][l, :, f * 128:(f + 1) * 128], 8, 128),
                                  w_part(w_up[which][l, :, DFF + f * 128:DFF + (f + 1) * 128], 8, 128)])
                for d in range(8):
                    slabs.append([w_part(w_dn[which][l, :, d * 128:(d + 1) * 128], NF, 128)])
            ffn(0)
            for s2 in range(2):
                slabs.append([w_part(w_in[l, :, s2 * 256:(s2 + 1) * 256], 8, 256)])
            for s2 in range(2):
                slabs.append([w_part(w_in[l, :, 512 + s2 * 256:512 + (s2 + 1) * 256], 8, 256)])
            for j in range(4):
                slabs.append([w_part(w_in[l, :, 1536 + j * 128:1536 + (j + 1) * 128], 8, 128),
                              w_part(w_in[l, :, 2048 + j * 128:2048 + (j + 1) * 128], 8, 128),
                              w_part(w_in[l, :, 1024 + j * 128:1024 + (j + 1) * 128], 8, 128)])
            for s2 in range(2):
                slabs.append([w_part(w_in[l, :, 2560 + s2 * 256:2560 + (s2 + 1) * 256], 8, 256)])
            for dp in range(4):
                slabs.append([w_part(a_out[l, :, dp * 256:(dp + 1) * 256], 4, 256),
                              w_part(b_out[l, :, dp * 256:(dp + 1) * 256], 4, 256),
                              w_part(c_out[l, :, dp * 256:(dp + 1) * 256], 4, 256)])
                for dd in range(2):
                    d = 2 * dp + dd
                    slabs.append([w_part(w_in[l, :, 3072 + br * 1024 + d * 128:3072 + br * 1024 + (d + 1) * 128], 8, 128)
                                  for br in range(3)])
            for s4 in range(4):
                slabs.append([w_part(w_o[l, :, s4 * 256:(s4 + 1) * 256], 8, 256)])
            ffn(1)
            for hf in range(2):
                slabs.append([w_part(w_pp[l, :, hf * 512:(hf + 1) * 512], 2, 512)])
                for s4 in (2 * hf, 2 * hf + 1):
                    slabs.append([w_part(w_pg[l, :, s4 * 256:(s4 + 1) * 256], 8, 256)])

        for _tile in range(2):
            for l in range(L):
                plan_layer(l)

        slab_slot = {}

        def slab_views(k):
            s = slab_slot[k]
            views, off = [], 0
            for (ap2d, kc, c) in slabs[k]:
                views.append(slots[s][:, off:off + kc * c].rearrange("p (k c) -> p k c", c=c))
                off += kc * c
            assert off <= SLOT_EL
            return views

        def issue_load(s):
            k = slab_state["next_load"]
            if k >= len(slabs):
                return
            slab_state["next_load"] = k + 1
            slab_slot[k] = s
            views = slab_views(k)
            parts = slabs[k]

            def emit(e, views=views, parts=parts):
                r = []
                for v, (ap2d, kc, c) in zip(views, parts):
                    r.append(e.dma_start(out=v, in_=ap2d.rearrange("(k p) c -> p k c", p=128)))
                return r
            S.add("pool", emit, writes=[("slot", s)], dma_sem=f"w{s}", ndma=len(parts), name=f"wload{k}")

        def consume():
            k = slab_state["next_use"]
            assert k < slab_state["next_load"], ("slab consumed before its load was issued", k)
            slab_state["next_use"] += 1
            return k, slab_views(k), ("slot", slab_slot[k])

        def release(k):
            issue_load(slab_slot[k])

        for s_ in range(NSLOT):
            issue_load(s_)

        def pe_mm(bank_i, out_ap, pairs, reads, name="mm"):
            def emit(e, out_ap=out_ap, pairs=pairs):
                last = None
                n = len(pairs)
                for i, (lt, rh) in enumerate(pairs):
                    last = e.matmul(out_ap, lt, rh, start=(i == 0), stop=(i == n - 1))
                return last
            S.add("pe", emit, reads=reads, writes=[("bank", bank_i)], name=name)

        def pe_multi(bank_i, groups_, reads, name="mmm"):
            def emit(e, groups_=groups_):
                last = None
                for out_ap, pairs in groups_:
                    n = len(pairs)
                    for i, (lt, rh) in enumerate(pairs):
                        last = e.matmul(out_ap, lt, rh, start=(i == 0), stop=(i == n - 1))
                return last
            S.add("pe", emit, reads=reads, writes=[("bank", bank_i)], name=name)

        def warm(n_mm):
            if n_mm <= 0:
                return
            b = free_banks[0][bank_ctr[0] % len(free_banks[0])]

            def emit(e, b=b):
                last = None
                for _ in range(n_mm):
                    last = e.matmul(banks[b][:, 0:128], ones_bf[:], ones_bf[:], start=True, stop=True)
                return last
            S.add("pe", emit, reads=["ones_bf"], writes=[("bank", b)], name="warm")

        def xres(g):
            return [("x", d, g[3]) for d in range(8)]

        stat_bank = {}
        pending_stats = []

        early_norm = [None]
        norm_done = [False]

        pending_en = [None]

        def maybe_early_norm(d, g):
            if d == 7 and early_norm[0] is not None:
                if pending_en[0] is not None:
                    flush_stats(pending_en[0][:2])
                    early_norm[0](pending_en[0])
                pending_en[0] = g

        def end_phase():
            if early_norm[0] is not None:
                warm(WARM_END)
            flush_stats()
            if pending_en[0] is not None and early_norm[0] is not None:
                early_norm[0](pending_en[0])
            pending_en[0] = None

        def post_x_update(d, g, scratch):
            c0, n, _, gid = g[:4]
            cs = slice(c0, c0 + n)
            gi = [q[:2] for q in groups_now].index(g[:2])
            sb_i = stat_bank[gi]
            if scratch == "h":
                sq_ap, sq_res = hT[:, d, cs], ("h", gid, d)
            else:
                sq_ap, sq_res = R[:, d, cs], ("R", d, gid)
            S.add("act", lambda e, sq_ap=sq_ap, d=d, cs=cs: e.activation(out=sq_ap, in_=xT[:, d, cs], func=AF.Square),
                  reads=[("x", d, gid)], writes=[sq_res], name="fsq")

            def do_stat(sq_ap=sq_ap, sq_res=sq_res, sb_i=sb_i, n=n, d=d):
                def emit(e):
                    return e.matmul(banks[sb_i][:, :n], ones_bf[:], sq_ap, start=(d == 0), stop=(d == 7))
                S.add("pe", emit, reads=[sq_res, "ones_bf"], writes=[("bank", sb_i)], name="fstat")
            pending_stats.append((g[:2], do_stat))
            if len(pending_stats) > 8:
                pending_stats.pop(0)[1]()

        def flush_stats(gkey=None):
            keep = []
            while pending_stats:
                k_, fn = pending_stats.pop(0)
                if gkey is None or k_ == gkey:
                    fn()
                else:
                    keep.append((k_, fn))
            pending_stats.extend(keep)

        def rmsnorm(g, gcol_fn, final=False, fused=False):
            c0, n, _, gid = g[:4]
            cs = slice(c0, c0 + n)
            if fused:
                b = stat_bank[[q[:2] for q in groups_now].index(g[:2])]
            else:
                def sq(e):
                    last = None
                    for c in range(8):
                        last = e.activation(out=hT[:, c, cs], in_=xT[:, c, cs], func=AF.Square)
                    return last
                S.add("act", sq, reads=xres(g), writes=[("h", gid)], name="nsq")
                warm(WARM_NSTAT)
                b = nb()
                pe_mm(b, banks[b][:, :n], [(ones_bf[:], hT[:, c, cs]) for c in range(8)], [("h", gid), "ones_bf"], "nstat")
            mi = rotate("ms", 1)
            if fused and groups_now and g[:2] == groups_now[0][:2]:
                S.add("act", lambda e: e.activation(out=nhalf[:, 0:1], in_=epsb[:, 0:1], func=AF.Ln), reads=["epsb"], writes=["nhalf_scr"], name="tblpre")
            S.add("act", lambda e: e.activation(out=msb[mi][:, :n], in_=banks[b][:, :n], func=AF.Ln, bias=epsb[:, 0:1], scale=1.0),
                  reads=[("bank", b), "epsb"], writes=[("ms", mi)], name="nln")
            ri = rotate("rstd", 1)
            S.add("act", lambda e: e.activation(out=rstd[ri][:, :n], in_=msb[mi][:, :n], func=AF.Exp, scale=-0.5),
                  reads=[("ms", mi)], writes=[("rstd", ri)], name="nexp")

            def apply(e):
                last = None
                for c in range(8):
                    out = xT[:, c, cs] if final else hT[:, c, cs]
                    last = e.scalar_tensor_tensor(out=out, in0=xT[:, c, cs], scalar=gcol_fn(c), in1=rstd[ri][:, :n],
                                                  op0=ALU.mult, op1=ALU.mult)
                return last
            if final:
                S.add("dve", apply, reads=xres(g) + [("rstd", ri), "vec"], writes=xres(g), name="napply")
            else:
                S.add("dve", apply, reads=xres(g) + [("rstd", ri), "vec"], writes=[("h", gid)], name="napply")

        def ffn(l, which, groups, interleave=(), fused_in=False):
            gbase = G_FFN1 if which == 0 else G_FFN2
            interleave = list(interleave)
            if not norm_done[0]:
                for g in groups:
                    rmsnorm(g, lambda c: vcol(l, gbase, c), fused=fused_in)
            warm(WARM_FFN)
            def up_item(f, g, wg, wu, sres):
                c0, n, _, gid = g[:4]
                cs = slice(c0, c0 + n)
                bg, bu = nb(), nb()
                pe_mm(bg, banks[bg][:, :n], [(wg[:, kk, :], hT[:, kk, cs]) for kk in range(8)], [sres, ("h", gid)], "up_g")
                pe_mm(bu, banks[bu][:, :n], [(wu[:, kk, :], hT[:, kk, cs]) for kk in range(8)], [sres, ("h", gid)], "up_u")
                ti = rotate("tmpA", 2)
                S.add("act", lambda e, ti=ti, bg=bg, n=n: e.activation(out=tmpA[ti][:, :n], in_=banks[bg][:, :n], func=AF.Silu),
                      reads=[("bank", bg)], writes=[("tmpA", ti)], name="silu")
                S.add("dve", lambda e, ti=ti, bu=bu, n=n, f=f, cs=cs: e.tensor_tensor(out=R[:, f, cs], in0=tmpA[ti][:, :n], in1=banks[bu][:, :n], op=ALU.mult),
                      reads=[("tmpA", ti), ("bank", bu)], writes=[("R", f, gid)], name="act")
            HEAD = 2
            head = [consume() for _ in range(HEAD)]
            for g in groups:
                for f in range(HEAD):
                    k, (wg, wu), sres = head[f]
                    up_item(f, g, wg, wu, sres)
            for f in range(HEAD):
                release(head[f][0])
            for f in range(HEAD, NF):
                k, (wg, wu), sres = consume()
                for g in groups:
                    up_item(f, g, wg, wu, sres)
                release(k)
            for d in range(8):
                k, (wd,), sres = consume()
                for g in groups:
                    c0, n, _, gid = g[:4]
                    cs = slice(c0, c0 + n)
                    b = nb()
                    pe_mm(b, banks[b][:, :n], [(wd[:, kk, :], R[:, kk, cs]) for kk in range(NF)],
                          [sres] + [("R", kk, gid) for kk in range(NF)], "down")
                    S.add("dve", lambda e, b=b, n=n, d=d, cs=cs: e.scalar_tensor_tensor(out=xT[:, d, cs], in0=banks[b][:, :n], scalar=0.5, in1=xT[:, d, cs],
                                                                                      op0=ALU.mult, op1=ALU.add),
                          reads=[("bank", b), ("x", d, gid)], writes=[("x", d, gid)], name="res")
                    post_x_update(d, g, "h")
                    maybe_early_norm(d, g)
                release(k)
                if interleave:
                    interleave.pop(0)()
            while interleave:
                interleave.pop(0)()
            end_phase()

        def load_p(l, tile, groups):
            blocks = [(pp[l, tile * 1024 + i * 128: tile * 1024 + (i + 1) * 128, :], i * 128, 128) for i in range(8)]
            if tile == 1:
                blocks.append((ps_[l, :, :], 1024, 64))
            return [(lambda src=src, t0=t0, m=m: load_p_block(src, t0, m)) for (src, t0, m) in blocks]

        def load_p_block(src, t0, m):
            if True:
                pi = rotate("pstage", 2)
                S.add("sp", lambda e, pi=pi, src=src, m=m: e.dma_start(out=pstage[pi][:m, :], in_=src), writes=[("pstage", pi)], dma_sem=f"pin{pi}", name="pload")
                b = nb()

                def tr(e, pi=pi, b=b, m=m):
                    last = None
                    for kk in range(2):
                        last = e.transpose(banks[b][:, kk * 128:kk * 128 + m], pstage[pi][:m, kk * 128:(kk + 1) * 128], ident[:m, :m])
                    return last
                S.add("pe", tr, reads=[("pstage", pi), "ident"], writes=[("bank", b)], name="ptr")
                S.add("act", lambda e, b=b, t0=t0, m=m: e.activation(out=pT[:, :, t0:t0 + m], in_=banks[b][:, 0:256].rearrange("p (k c) -> p k c", c=128)[:, :, :m], func=AF.Copy),
                      reads=[("bank", b)], writes=[("pT", t0)], name="pcopy")

        def ptreads(g):
            c0, n = g[0], g[1]
            return [("pT", t0) for t0 in range(c0, c0 + n, 128)] if n >= 128 else [("pT", c0)]

        def ple(l, tile, groups):
            if not norm_done[0]:
                for g in groups:
                    rmsnorm(g, lambda c: vcol(l, G_PLE, c), fused=True)
            warm(WARM_PLE)
            for s4 in range(4):
                if s4 % 2 == 0:
                    kp, (wp,), pres = consume()
                k, (wg,), sres = consume()
                for g in groups:
                    for dd in range(2):
                        d = 2 * s4 + dd
                        c0, n, _, gid = g[:4]
                        cs = slice(c0, c0 + n)
                        bg, bp = nb(), nb()
                        pe_mm(bg, banks[bg][:, :n], [(wg[:, kk, dd * 128:(dd + 1) * 128], hT[:, kk, cs]) for kk in range(8)], [sres, ("h", gid)], "pg")
                        pe_mm(bp, banks[bp][:, :n], [(wp[:, kk, (d % 4) * 128:(d % 4 + 1) * 128], pT[:, kk, cs]) for kk in range(2)], [pres] + ptreads(g), "pp")
                        ti = rotate("tmpA", 2)
                        S.add("act", lambda e, ti=ti, bg=bg, n=n: e.activation(out=tmpA[ti][:, :n], in_=banks[bg][:, :n], func=AF.Sigmoid),
                              reads=[("bank", bg)], writes=[("tmpA", ti)], name="psig")
                        tb_ = rotate("tmpB", 2)

                        S.add("dve", lambda e, ti=ti, tb_=tb_, bp=bp, n=n: e.tensor_tensor(out=tmpB[tb_][:, :n], in0=tmpA[ti][:, :n], in1=banks[bp][:, :n], op=ALU.mult),
                              reads=[("tmpA", ti), ("bank", bp)], writes=[("tmpB", tb_)], name="pmul")
                        S.add("dve", lambda e, tb_=tb_, n=n, d=d, cs=cs: e.tensor_tensor(out=xT[:, d, cs], in0=xT[:, d, cs], in1=tmpB[tb_][:, :n], op=ALU.add),
                              reads=[("tmpB", tb_), ("x", d, gid)], writes=[("x", d, gid)], name="padd")
                        post_x_update(d, g, "R")
                        maybe_early_norm(d, g)
                release(k)
                if s4 % 2 == 1:
                    release(kp)
            end_phase()

        def load_state(l):
            def ld(e):
                return [e.dma_start(out=stage[0][0:32, 0:512], in_=sconv[l]),
                        e.dma_start(out=stage[0][0:120, 512:1024], in_=spool[l, 0:120, :]),
                        e.dma_start(out=stage[1][0:120, 0:512], in_=spool[l, 120:240, :])]
            S.add("sp", ld, writes=[("stage", 0), ("stage", 1)], dma_sem="state", ndma=3, name="stload")
            for (si, c_lo, m, kind, off) in ((0, 0, 32, "c", 0), (0, 512, 120, "p", 0), (1, 0, 120, "p", 120)):
                b = nb()

                def tr(e, b=b, si=si, c_lo=c_lo, m=m):
                    last = None
                    for j in range(4):
                        last = e.transpose(banks[b][:, j * 128:j * 128 + m], stage[si][:m, c_lo + j * 128:c_lo + (j + 1) * 128], ident[:m, :m])
                    return last
                S.add("pe", tr, reads=[("stage", si), "ident"], writes=[("bank", b)], name="sttr")
                src = banks[b][:, :].rearrange("p (j c) -> p j c", c=128)[:, :, :m]
                if kind == "c":
                    S.add("act", lambda e, src=src: e.activation(out=cins[:, :, 0:32], in_=src, func=AF.Copy),
                          reads=[("bank", b)], writes=["cins_h"], name="stc")
                else:
                    S.add("act", lambda e, src=src, off=off, m=m: e.activation(out=xcs[:, :, off:off + m], in_=src, func=AF.Copy),
                          reads=[("bank", b)], writes=[("xcs_h", off)], name="stp")
            S.add("sp", lambda e: e.dma_start(out=pool_s[l, 0:176, :], in_=spool[l, 64:240, :]), dma_sem="d2d", name="pool_d2d")

        MERGED = [16, 17, 18, 19, 20, 21, 0, 1]

        def mixer(l, tile, groups, mgroups):
            pgroups = [g for g in groups if g[2] == "p"]
            sgroups = [g for g in groups if g[2] == "s"]
            if not norm_done[0]:
                for g in mgroups:
                    rmsnorm(g, lambda c: vcol(l, G_MIX, c), fused=True)
            warm(WARM_MIX)
            if KDEBUG and tile == 1 and l == 0:
                S.add("sp", lambda e: e.dma_start(out=dbg_h, in_=hT[:, :, 1024:1088]), reads=[("h", groups[2][3])], dma_sem="dbg2", name="dbgh")
            if sgroups:
                load_state(l)
            uslabs = [consume(), consume()]
            for g in mgroups:
                c0, n, _, gid = g[:4]
                cs = slice(c0, c0 + n)
                for j in range(4):
                    k, (wu,), sres = uslabs[j // 2]
                    jj = j % 2
                    b = nb()
                    pe_mm(b, banks[b][:, :n], [(wu[:, kk, jj * 128:(jj + 1) * 128], hT[:, kk, cs]) for kk in range(8)], [sres, ("h", gid)], "u")
                    S.add("act", lambda e, b=b, n=n, j=j, cs=cs: e.activation(out=R[:, j, cs], in_=banks[b][:, :n], func=AF.Gelu_apprx_tanh),
                          reads=[("bank", b)], writes=[("R", j, gid)], name="ugelu")
            release(uslabs[0][0])
            release(uslabs[1][0])
            k0, (wv0,), sres0 = consume()
            k1, (wv1,), sres1 = consume()
            blocks = []
            for g in pgroups:
                for i in range(g[1] // 128):
                    blocks.append((g[0] + i * 128, 128, (g[0] + i * 128) // 128, g[3], False))
            for g in sgroups:
                blocks.append((g[0], 64, 8, g[3], True))
            for (t0, m, tb, gid, is_s) in blocks:
                b = nb()
                pe_multi(b, [(banks[b][:m, 0:256], [(hT[:, kk, t0:t0 + m], wv0[:, kk, :]) for kk in range(8)]),
                             (banks[b][:m, 256:512], [(hT[:, kk, t0:t0 + m], wv1[:, kk, :]) for kk in range(8)])],
                         [sres0, sres1, ("h", gid)], "v")
                dst = vg_s[:m, :] if is_s else vbf[:m, tb, :]
                dres = "vg_s" if is_s else ("v", tb)
                ti = rotate("tmpA", 2)
                S.add("act", lambda e, b=b, m=m, dst=dst, tb=tb: e.activation(out=dst, in_=banks[b][:m, :], func=AF.Gelu_apprx_tanh, accum_out=vstat[:m, tb, 0:1]),
                      reads=[("bank", b)], writes=[dres, "vstat01"], name="vgelu")
                S.add("act", lambda e, m=m, dst=dst, tb=tb, ti=ti: e.activation(out=tmpA[ti][:m, :], in_=dst, func=AF.Square, accum_out=vstat[:m, tb, 1:2]),
                      reads=[dres], writes=[("tmpA", ti), "vstat01"], name="vsq")
            release(k0)
            release(k1)
            S.add("dve", lambda e: e.tensor_scalar(out=vstat[:, :, 2], in0=vstat[:, :, 0], scalar1=1.0 / 512.0, scalar2=None, op0=ALU.mult),
                  reads=["vstat01"], writes=["vs2"], name="vmean")
            S.add("dve", lambda e: e.tensor_tensor(out=vstat[:, :, 3], in0=vstat[:, :, 0], in1=vstat[:, :, 0], op=ALU.mult),
                  reads=["vstat01"], writes=["vs3"], name="vs1sq")
            S.add("dve", lambda e: e.tensor_scalar(out=vstat[:, :, 6], in0=vstat[:, :, 3], scalar1=-1.0 / (512.0 * 512.0), scalar2=EPS, op0=ALU.mult, op1=ALU.add),
                  reads=["vs3"], writes=["vs6"], name="vnm")
            S.add("dve", lambda e: e.scalar_tensor_tensor(out=vstat[:, :, 4], in0=vstat[:, :, 1], scalar=1.0 / 512.0, in1=vstat[:, :, 6], op0=ALU.mult, op1=ALU.add),
                  reads=["vstat01", "vs6"], writes=["vs4"], name="vvar")
            S.add("act", lambda e: e.activation(out=vstat[:, :, 7], in_=vstat[:, :, 4], func=AF.Ln), reads=["vs4"], writes=["vs7"], name="vln")
            S.add("act", lambda e: e.activation(out=vstat[:, :, 5], in_=vstat[:, :, 7], func=AF.Exp, scale=-0.5), reads=["vs7"], writes=["vs5"], name="vexp")
            S.add("dve", lambda e: e.scalar_tensor_tensor(out=vstat[:, :, 6], in0=vstat[:, :, 2], scalar=-1.0, in1=vstat[:, :, 5], op0=ALU.mult, op1=ALU.mult),
                  reads=["vs2", "vs5"], writes=["vs6"], name="vnmr")
            v3_pending = []

            def v3_block(t0, m, tb, gid, is_s):
                src = vg_s[:m, :] if is_s else vbf[:m, tb, :]
                dres = "vg_s" if is_s else ("v", tb)
                S.add("act", lambda e, m=m, src=src, tb=tb: e.activation(out=vtmp[:m, :], in_=src, func=AF.Identity, scale=vstat[:m, tb, 5:6], bias=vstat[:m, tb, 6:7]),
                      reads=[dres, "vs5", "vs6"], writes=["vtmp"], name="vn1")
                S.add("pool", lambda e, m=m: e.tensor_tensor(out=vtmp[:m, :], in0=vtmp[:m, :], in1=lng[:m, l, :], op=ALU.mult),
                      reads=["vtmp", "ln"], writes=["vtmp"], name="vn2")
                if is_s:
                    S.add("pool", lambda e, m=m: e.tensor_tensor(out=vs_f[:m, :], in0=vtmp[:m, :], in1=lnb[:m, l, :], op=ALU.add),
                          reads=["vtmp", "ln"], writes=["vs_f"], name="vn3s")
                    S.add("pool", lambda e, m=m, tb=tb: e.tensor_copy(out=vbf[:m, tb, :], in_=vs_f[:m, :]), reads=["vs_f"], writes=[("v", tb)], name="vn4s")
                    S.add("sp", lambda e: e.dma_start(out=va_s[l], in_=vs_f[:, :]), reads=["vs_f"], dma_sem="o_v", name="va_out")
                else:
                    S.add("pool", lambda e, m=m, tb=tb: e.tensor_tensor(out=vbf[:m, tb, :], in0=vtmp[:m, :], in1=lnb[:m, l, :], op=ALU.add),
                          reads=["vtmp", "ln"], writes=[("v", tb)], name="vn3")
            for blk in blocks:
                v3_pending.append(lambda blk=blk: v3_block(*blk))
            warm(WARM_CONV)
            for j in range(4):
                k, (wc, wx, wb_), sres = consume()
                if tile == 0:
                    S.add("dve", lambda e: e.memset(cinp[:, 0:2], 0.0), writes=["cinp_h"], name="cz")
                else:
                    S.add("dve", lambda e, j=j: e.tensor_copy(out=cinp[:, 0:2], in_=hist_c[:, l, j, :]), reads=[("hist_c", l, j)], writes=["cinp_h"], name="ch")
                for g in groups:
                    c0, n, kind, gid = g[:4]
                    cs = slice(c0, c0 + n)
                    bc, bx, bb = nb(), nb(), nb()
                    pe_mm(bc, banks[bc][:, :n], [(wc[:, kk, :], hT[:, kk, cs]) for kk in range(8)], [sres, ("h", gid)], "cg")
                    pe_mm(bx, banks[bx][:, :n], [(wx[:, kk, :], hT[:, kk, cs]) for kk in range(8)], [sres, ("h", gid)], "xb")
                    pe_mm(bb, banks[bb][:, :n], [(wb_[:, kk, :], hT[:, kk, cs]) for kk in range(8)], [sres, ("h", gid)], "bg")
                    ti = rotate("tmpA", 2)
                    S.add("act", lambda e, ti=ti, bc=bc, n=n: e.activation(out=tmpA[ti][:, :n], in_=banks[bc][:, :n], func=AF.Copy),
                          reads=[("bank", bc)], writes=[("tmpA", ti)], name="ccopy")
                    tb_ = rotate("tmpB", 2)
                    if kind == "p":
                        buf, o = cinp, 2 + c0
                        sh = 1
                        hres = ["cinp_h", ("cinp", g[4] - 1)]
                        wres = [("cinp", g[4])]
                    else:
                        buf, o = cins[:, j, :], 32
                        sh = 16
                        hres = ["cins_h"]
                        wres = [("cins", j)]

                    B_ = ("tmpB", tb_)
                    S.add("dve", lambda e, ti=ti, bx=bx, n=n, buf=buf, o=o: e.tensor_tensor(out=buf[:, o:o + n], in0=tmpA[ti][:, :n], in1=banks[bx][:, :n], op=ALU.mult),
                          reads=[("tmpA", ti), ("bank", bx)], writes=wres, name="cin")
                    S.add("dve", lambda e, tb_=tb_, n=n, buf=buf, o=o, j=j: e.tensor_scalar(out=tmpB[tb_][:, :n], in0=buf[:, o:o + n], scalar1=vcol(l, B_CONV, 2 * 4 + j), scalar2=None, op0=ALU.mult),
                          reads=wres + ["vec"], writes=[B_], name="tap2")
                    S.add("dve", lambda e, tb_=tb_, n=n, buf=buf, o=o, sh=sh, j=j: e.scalar_tensor_tensor(
                        out=tmpB[tb_][:, :n], in0=buf[:, o - sh:o - sh + n], scalar=vcol(l, B_CONV, 1 * 4 + j), in1=tmpB[tb_][:, :n], op0=ALU.mult, op1=ALU.add),
                        reads=wres + hres + [B_, "vec"], writes=[B_], name="tap1")
                    S.add("dve", lambda e, tb_=tb_, n=n, buf=buf, o=o, sh=sh, j=j: e.scalar_tensor_tensor(
                        out=tmpB[tb_][:, :n], in0=buf[:, o - 2 * sh:o - 2 * sh + n], scalar=vcol(l, B_CONV, 0 * 4 + j), in1=tmpB[tb_][:, :n], op0=ALU.mult, op1=ALU.add),
                        reads=wres + hres + [B_, "vec"], writes=[B_], name="tap0")
                    S.add("dve", lambda e, tb_=tb_, bb=bb, n=n, j=j, cs=cs: e.tensor_tensor(out=R[:, 8 + j, cs], in0=tmpB[tb_][:, :n], in1=banks[bb][:, :n], op=ALU.mult),
                          reads=[B_, ("bank", bb)], writes=[("R", 8 + j, gid)], name="ybin")
                    if v3_pending:
                        v3_pending.pop(0)()
                lastg = pgroups[-1]
                S.add("dve", lambda e, j=j: e.tensor_copy(out=hist_c[:, l, j, :], in_=cinp[:, 1024:1026]),
                      reads=[("cinp", lastg[4])], writes=[("hist_c", l, j)], name="chs")
                release(k)
            while v3_pending:
                v3_pending.pop(0)()
            pending_pc = []
            for s2 in range(2):
                k, (wxc,), sres = consume()
                for jj in range(2):
                    j = 2 * s2 + jj
                    w = 2 ** (j + 1)
                    if tile == 0:
                        S.add("dve", lambda e: e.memset(xcb[:, 0:15], 0.0), writes=["xcb_h"], name="pz")
                    else:
                        S.add("dve", lambda e, j=j: e.tensor_copy(out=xcb[:, 0:15], in_=hist_p[:, l, j, :]), reads=[("hist_p", l, j)], writes=["xcb_h"], name="ph")
                    for g in groups:
                        c0, n, kind, gid = g[:4]
                        cs = slice(c0, c0 + n)
                        b = nb()
                        pe_mm(b, banks[b][:, :n], [(wxc[:, kk, jj * 128:(jj + 1) * 128], hT[:, kk, cs]) for kk in range(8)], [sres, ("h", gid)], "xc")
                        if kind == "p":
                            S.add("act", lambda e, b=b, n=n, c0=c0: e.activation(out=xcb[:, 15 + c0:15 + c0 + n], in_=banks[b][:, :n], func=AF.Copy),
                                  reads=[("bank", b)], writes=[("xcb", g[4])], name="xccopy")
                            xin = xcb[:, c0:c0 + n + 15]
                            tot = n + 15
                            sh = 1
                            o = 15
                            rres = ["xcb_h", ("xcb", g[4] - 1), ("xcb", g[4])]
                        else:
                            S.add("act", lambda e, b=b, j=j: e.activation(out=xcs[:, j, 240:304], in_=banks[b][:, :64], func=AF.Copy),
                                  reads=[("bank", b)], writes=[("xcs", j)], name="xccopy_s")
                            xin = xcs[:, j, :]
                            tot = 304
                            sh = 16
                            o = 240
                            rres = [("xcs_h", 0), ("xcs_h", 120), ("xcs", j)]
                        pi = rotate("pooled", 4)
                        first = (tile == 0 and kind == "p" and c0 == 0)

                        cur, cres = xin, list(rres)
                        for m_ in range(1, j + 2):
                            kk = (2 ** (m_ - 1)) * sh
                            lo = (2 ** m_ - 1) * sh
                            li = (m_ - 1) % 2
                            dst = lev[li][:, 0:tot]
                            S.add("dve", lambda e, dst=dst, cur=cur, lo=lo, kk=kk, tot=tot: e.tensor_tensor(out=dst[:, lo:tot], in0=cur[:, lo:tot], in1=cur[:, lo - kk:tot - kk], op=ALU.add),
                                  reads=cres, writes=[("lev", li)], name="plev")
                            cur, cres = dst, [("lev", li)]
                        if first:
                            lo_i = (j + 1) % 2
                            tbf = lev[lo_i]
                            S.add("dve", lambda e, n=n, cur=cur, o=o, w=w, xin=xin, pi=pi: e.scalar_tensor_tensor(
                                out=pooled[pi][:, 16:n], in0=cur[:, o + 16:o + n], scalar=1.0 / w, in1=xin[:, o + 16:o + n], op0=ALU.mult, op1=ALU.subtract),
                                reads=cres + rres, writes=[("pooled", pi)], name="pfin")
                            S.add("dve", lambda e, tbf=tbf, cur=cur, o=o, j=j: e.tensor_tensor(out=tbf[:, 0:16], in0=cur[:, o:o + 16], in1=invcnt[:, j, :], op=ALU.mult),
                                  reads=cres + ["invcnt"], writes=[("lev", lo_i)], name="pfix1")
                            S.add("dve", lambda e, tbf=tbf, xin=xin, o=o, pi=pi: e.tensor_tensor(out=pooled[pi][:, 0:16], in0=tbf[:, 0:16], in1=xin[:, o:o + 16], op=ALU.subtract),
                                  reads=[("lev", lo_i)] + rres, writes=[("pooled16", pi)], name="pfix2")
                            pres_ = [("pooled", pi), ("pooled16", pi)]
                        else:
                            S.add("dve", lambda e, n=n, cur=cur, o=o, w=w, xin=xin, pi=pi: e.scalar_tensor_tensor(
                                out=pooled[pi][:, :n], in0=cur[:, o:o + n], scalar=1.0 / w, in1=xin[:, o:o + n], op0=ALU.mult, op1=ALU.subtract),
                                reads=cres + rres, writes=[("pooled", pi), ("pooled16", pi)], name="pfin")
                            pres_ = [("pooled", pi), ("pooled16", pi)]
                        def do_pc(n=n, j=j, cs=cs, pi=pi, pres_=pres_, gid=gid):
                            b2 = nb()
                            pe_mm(b2, banks[b2][:, :n], [(cw_b[:, l, j, :], pooled[pi][:, :n])], pres_ + ["cw"], "pc")
                            S.add("act", lambda e, b2=b2, n=n, j=j, cs=cs: e.activation(out=R[:, 12 + j, cs], in_=banks[b2][:, :n], func=AF.Copy, scale=vcol(l, C_SCALE, j)),
                                  reads=[("bank", b2), "vec"], writes=[("R", 12 + j, gid)], name="pcs")
                        pending_pc.append(do_pc)
                        if len(pending_pc) > 3:
                            pending_pc.pop(0)()
                    lastg = pgroups[-1]
                    S.add("dve", lambda e, j=j: e.tensor_copy(out=hist_p[:, l, j, :], in_=xcb[:, 1024:1039]),
                          reads=[("xcb", lastg[4])], writes=[("hist_p", l, j)], name="phs")
                release(k)
            while pending_pc:
                pending_pc.pop(0)()
            if tile == 1:
                b, b2_ = nb(), nb()

                def trc(e, b=b, b2_=b2_):
                    last = None
                    for j in range(4):
                        e.transpose(banks[b][0:2, j * 128:(j + 1) * 128], hist_c[:, l, j, :], ident[:, :])
                    for j in range(4):
                        last = e.transpose(banks[b2_][0:32, j * 128:(j + 1) * 128], cins[:, j, 64:96], ident[:, :])
                    return last
                S.add("pe", trc, reads=[("hist_c", l, j) for j in range(4)] + [("cins", j) for j in range(4)] + ["ident"],
                      writes=[("bank", b), ("bank", b2_)], name="trc")

                def cpc(e, b=b, b2_=b2_):
                    e.activation(out=stage[1][0:2, 0:512], in_=banks[b][0:2, :], func=AF.Copy)
                    return e.activation(out=stage[1][0:32, 512:1024], in_=banks[b2_][0:32, :], func=AF.Copy)
                S.add("act", cpc, reads=[("bank", b), ("bank", b2_)], writes=[("stage", 1), ("stageb", 1)], name="cpc")
                S.add("sp", lambda e: [e.dma_start(out=conv_p[l], in_=stage[1][0:2, 0:512]), e.dma_start(out=conv_s[l], in_=stage[1][0:32, 512:1024])],
                      reads=[("stage", 1), ("stageb", 1)], dma_sem="o_small", ndma=2, name="conv_out")
            if tile == 1:
                b, b2_ = nb(), nb()

                def trp(e, b=b, b2_=b2_):
                    last = None
                    for j in range(4):
                        e.transpose(banks[b][0:15, j * 128:(j + 1) * 128], hist_p[:, l, j, :], ident[:, :])
                    for j in range(4):
                        last = e.transpose(banks[b2_][0:64, j * 128:(j + 1) * 128], xcs[:, j, 240:304], ident[:, :])
                    return last
                S.add("pe", trp, reads=[("hist_p", l, j) for j in range(4)] + [("xcs", j) for j in range(4)] + ["ident"],
                      writes=[("bank", b), ("bank", b2_)], name="trp")

                def cpp(e, b=b, b2_=b2_):
                    e.activation(out=stage[1][0:15, 0:512], in_=banks[b][0:15, :], func=AF.Copy)
                    return e.activation(out=stage[1][0:64, 512:1024], in_=banks[b2_][0:64, :], func=AF.Copy)
                S.add("act", cpp, reads=[("bank", b), ("bank", b2_)], writes=[("stage", 1), ("stageb", 1)], name="cpp")
                S.add("sp", lambda e: [e.dma_start(out=pool_p[l], in_=stage[1][0:15, 0:512]), e.dma_start(out=pool_s[l, 176:240, :], in_=stage[1][0:64, 512:1024])],
                      reads=[("stage", 1), ("stageb", 1)], dma_sem="o_small", ndma=2, name="pool_out")
            for g in groups:
                c0, n, kind, gid = g[:4]
                cs = slice(c0, c0 + n)
                for j in range(4):
                    b = nb()
                    if kind == "p":
                        grp = [(banks[b][:, i * 128:(i + 1) * 128], [(vbf[:, c0 // 128 + i, j * 128:(j + 1) * 128], wsT_b[:, l * 4 + j, :])]) for i in range(n // 128)]
                        pe_multi(b, grp, [("v", c0 // 128 + i) for i in range(n // 128)] + ["wsT_b"], "sgate")
                        bias_ap = bias4[:, l, j * 128:(j + 1) * 128]
                    else:
                        pe_mm(b, banks[b][:, :64], [(vbf[:64, 8, j * 128:(j + 1) * 128], bd_b[:, l * 4 + j, :])], [("v", 8), "bd_b"], "sgate_s")
                        bias_ap = biass[:, l, j * 64:(j + 1) * 64]
                    tb_ = rotate("tmpB", 2)

                    if kind == "p":
                        S.add("dve", lambda e, b=b, n=n, tb_=tb_, bias_ap=bias_ap: e.tensor_tensor(
                            out=tmpB[tb_][:, :n].rearrange("p (r t) -> p r t", t=128), in0=banks[b][:, :n].rearrange("p (r t) -> p r t", t=128),
                            in1=bias_ap.unsqueeze(1).to_broadcast([128, n // 128, 128]), op=ALU.add),
                            reads=[("bank", b), "bias"], writes=[("tmpB", tb_)], name="sgb")
                    else:
                        S.add("dve", lambda e, b=b, n=n, tb_=tb_, bias_ap=bias_ap: e.tensor_tensor(out=tmpB[tb_][:, :n], in0=banks[b][:, :n], in1=bias_ap, op=ALU.add),
                              reads=[("bank", b), "bias"], writes=[("tmpB", tb_)], name="sgb")
                    S.add("dve", lambda e, n=n, tb_=tb_, j=j, cs=cs: e.tensor_tensor(out=R[:, 4 + j, cs], in0=tmpB[tb_][:, :n], in1=R[:, j, cs], op=ALU.mult),
                          reads=[("tmpB", tb_), ("R", j, gid)], writes=[("R", 4 + j, gid)], name="sgmul")
            for dp in range(4):
                ko, outs_w, ores = consume()
                for dd in range(2):
                    d = 2 * dp + dd
                    kg, gates_w, gres = consume()
                    for g in mgroups:
                        c0, n, kind, gid = g[:4]
                        cs = slice(c0, c0 + n)
                        acc_i = rotate("tmpB", 2)
                        for br in range(3):
                            bgt, by = nb(), nb()
                            pe_mm(bgt, banks[bgt][:, :n], [(gates_w[br][:, kk, :], hT[:, kk, cs]) for kk in range(8)], [gres, ("h", gid)], "gate")
                            pe_mm(by, banks[by][:, :n], [(outs_w[br][:, kk, dd * 128:(dd + 1) * 128], R[:, 4 + 4 * br + kk, cs]) for kk in range(4)],
                                  [ores] + [("R", 4 + 4 * br + kk, gid) for kk in range(4)], "ybr")
                            ti = rotate("tmpA", 2)
                            S.add("act", lambda e, ti=ti, bgt=bgt, n=n: e.activation(out=tmpA[ti][:, :n], in_=banks[bgt][:, :n], func=AF.Sigmoid),
                                  reads=[("bank", bgt)], writes=[("tmpA", ti)], name="gsig")

                            A_, ACC = ("tmpA", ti), ("tmpB", acc_i)
                            if br == 0:
                                S.add("dve", lambda e, ti=ti, by=by, n=n, acc_i=acc_i: e.tensor_tensor(out=tmpB[acc_i][:, :n], in0=tmpA[ti][:, :n], in1=banks[by][:, :n], op=ALU.mult),
                                      reads=[A_, ("bank", by)], writes=[ACC], name="mg0")
                            else:
                                S.add("dve", lambda e, ti=ti, by=by, n=n: e.tensor_tensor(out=tmpA[ti][:, :n], in0=tmpA[ti][:, :n], in1=banks[by][:, :n], op=ALU.mult),
                                      reads=[A_, ("bank", by)], writes=[A_], name="mgm")
                                if br == 1:
                                    S.add("dve", lambda e, ti=ti, n=n, acc_i=acc_i: e.tensor_tensor(out=tmpB[acc_i][:, :n], in0=tmpB[acc_i][:, :n], in1=tmpA[ti][:, :n], op=ALU.add),
                                          reads=[A_, ACC], writes=[ACC], name="mga")
                                else:
                                    S.add("dve", lambda e, ti=ti, n=n, acc_i=acc_i, d=d, cs=cs: e.tensor_tensor(out=R[:, MERGED[d], cs], in0=tmpB[acc_i][:, :n], in1=tmpA[ti][:, :n], op=ALU.add),
                                          reads=[A_, ACC], writes=[("R", MERGED[d], gid)], name="mgf")
                    release(kg)
                release(ko)
            for s4 in range(4):
                k, (wo,), sres = consume()
                for dd in range(2):
                    d = 2 * s4 + dd
                    for g in mgroups:
                        c0, n, _, gid = g[:4]
                        cs = slice(c0, c0 + n)
                        b = nb()
                        pe_mm(b, banks[b][:, :n], [(wo[:, kk, dd * 128:(dd + 1) * 128], R[:, MERGED[kk], cs]) for kk in range(8)],
                              [sres] + [("R", MERGED[kk], gid) for kk in range(8)], "wo")
                        S.add("dve", lambda e, b=b, n=n, d=d, cs=cs: e.tensor_tensor(out=xT[:, d, cs], in0=xT[:, d, cs], in1=banks[b][:, :n], op=ALU.add),
                              reads=[("bank", b), ("x", d, gid)], writes=[("x", d, gid)], name="wores")
                        post_x_update(d, g, "h")
                        maybe_early_norm(d, g)
                release(k)
            end_phase()

        def load_x(tile, groups):
            blocks = [(xp[tile * 1024 + i * 128: tile * 1024 + (i + 1) * 128, :], i * 128, 128, i // 4) for i in range(8)]
            if tile == 1:
                blocks.append((xs[:, :], 1024, 64, 2))
            for (src, t0, m, gi) in blocks:
                gid = gran(t0, m)
                si = rotate("stage", 2)
                S.add("sp", lambda e, si=si, src=src, m=m: e.dma_start(out=stage[si][:m, :], in_=src), writes=[("stage", si)], dma_sem=f"sin{si}", name="xload")
                warm(WARM_XLD)
                for hh in range(2):
                    b = nb()

                    def tr(e, si=si, b=b, m=m, hh=hh):
                        last = None
                        for c in range(4):
                            last = e.transpose(banks[b][:, c * 128:c * 128 + m], stage[si][:m, (4 * hh + c) * 128:(4 * hh + c + 1) * 128], ident[:m, :m])
                        return last
                    S.add("pe", tr, reads=[("stage", si), "ident"], writes=[("bank", b)], name="xtr")
                    src_v = banks[b][:, :].rearrange("p (c t) -> p c t", t=128)[:, :, :m]
                    if hh == 0:
                        S.add("act", lambda e, src_v=src_v, t0=t0, m=m, hh=hh: e.activation(out=xT[:, 4 * hh:4 * hh + 4, t0:t0 + m], in_=src_v, func=AF.Copy),
                              reads=[("bank", b)], writes=[("x", d, gid) for d in range(4)], name="xcp")
                    else:
                        S.add("dve", lambda e, src_v=src_v, t0=t0, m=m, hh=hh: e.tensor_copy(out=xT[:, 4 * hh:4 * hh + 4, t0:t0 + m], in_=src_v),
                              reads=[("bank", b)], writes=[("x", d, gid) for d in range(4, 8)], name="xcp")

        def store_y(tile, groups):
            OFF = L * NV
            if not norm_done[0]:
                for g in groups:
                    rmsnorm(g, lambda c: vec_sb[:, OFF + c:OFF + c + 1], final=True, fused=True)
            warm(WARM_FFN)
            blocks = [(yp[tile * 1024 + i * 128: tile * 1024 + (i + 1) * 128, :], i * 128, 128, i // 4) for i in range(8)]
            if tile == 1:
                blocks.append((ys[:, :], 1024, 64, 2))
            for (dst, t0, m, gi) in blocks:
                g = (t0, m, None, gran(t0, m))
                si = rotate("stage", 2)
                for hh in range(2):
                    b = nb()

                    def tr(e, b=b, m=m, hh=hh, t0=t0):
                        last = None
                        for c in range(4):
                            last = e.transpose(banks[b][:m, c * 128:(c + 1) * 128], xT[:, 4 * hh + c, t0:t0 + m], ident[:, :])
                        return last
                    S.add("pe", tr, reads=xres(g) + ["ident"], writes=[("bank", b)], name="ytr")
                    if hh == 0:
                        S.add("act", lambda e, b=b, m=m, si=si: e.activation(out=stage[si][:m, 0:512], in_=banks[b][:m, :], func=AF.Copy),
                              reads=[("bank", b)], writes=[("stage", si)], name="ycp")
                    else:
                        S.add("dve", lambda e, b=b, m=m, si=si: e.tensor_copy(out=stage[si][:m, 512:1024], in_=banks[b][:m, :]),
                              reads=[("bank", b)], writes=[("stageb", si)], name="ycp")
                S.add("sp", lambda e, si=si, dst=dst, m=m: e.dma_start(out=dst, in_=stage[si][:m, :]), reads=[("stage", si), ("stageb", si)],
                      writes=[("stage", si)], dma_sem=f"sout{si}", name="ystore")

        groups_now = []
        for tile in range(2):
            groups = [(0, 512, "p", gran(0, 512), 0), (512, 512, "p", gran(512, 512), 1)]
            mgroups = list(groups)
            if tile == 1:
                groups.append((1024, 64, "s", gran(1024, 64), 2))
                mgroups = [(0, 384, "m", gran(0, 384), 0), (384, 384, "m", gran(384, 384), 1), (768, 320, "m", gran(768, 320), 2)]
            groups_now[:] = mgroups
            stat_bank.clear()
            for gi in range(len(mgroups)):
                stat_bank[gi] = 7 - gi
            free_banks[0] = list(range(8 - len(mgroups)))
            load_x(tile, groups)

            def dumpx(i, groups=groups, tile=tile):
                if KDEBUG and tile == 0:
                    S.add("sp", lambda e: e.dma_start(out=dbg_xs[i], in_=xT[:, :, 0:64]), reads=xres(groups[0]), dma_sem=f"dx{i}", name="dumpx")
            OFFF = L * NV

            def mk(fn_col, final=False):
                return lambda g: rmsnorm(g, fn_col, final=final, fused=True)
            norm_done[0] = False
            for l in range(L):
                if EARLY_NORM:
                    early_norm[0] = mk(lambda c, l=l: vcol(l, G_MIX, c))
                ffn(l, 0, mgroups, interleave=load_p(l, tile, groups), fused_in=(l > 0))
                norm_done[0] = EARLY_NORM
                dumpx(4 * l + 0)
                if EARLY_NORM:
                    early_norm[0] = mk(lambda c, l=l: vcol(l, G_FFN2, c))
                mixer(l, tile, groups, mgroups)
                dumpx(4 * l + 1)
                if EARLY_NORM:
                    early_norm[0] = mk(lambda c, l=l: vcol(l, G_PLE, c))
                ffn(l, 1, mgroups, fused_in=True)
                dumpx(4 * l + 2)
                if EARLY_NORM:
                    if l + 1 < L:
                        early_norm[0] = mk(lambda c, l=l: vcol(l + 1, G_FFN1, c))
                    else:
                        early_norm[0] = mk(lambda c: vec_sb[:, OFFF + c:OFFF + c + 1], final=True)
                ple(l, tile, mgroups)
                dumpx(4 * l + 3)
            early_norm[0] = None
            store_y(tile, mgroups)
            norm_done[0] = False
        assert slab_state["next_use"] == len(slabs), (slab_state, len(slabs))

        S.finalize()
        run = S.runner(sems, dsems)
        with nc.Block() as block:
            @block.tensor
            def _(e):
                run("pe", e)

            @block.scalar
            def _(e):
                run("act", e)

            @block.vector
            def _(e):
                run("dve", e)

            @block.gpsimd
            def _(e):
                run("pool", e)

            @block.sync
            def _(e):
                run("sp", e)
                for n_, c_ in S.dma_counts.items():
                    e.wait_ge(dsems[n_], c_)
    return nc


def _prep_inputs(inp):
    f = lambda a: np.ascontiguousarray(np.asarray(a, dtype=np.float32))
    a_ws = f(inp["a_ws"])
    a_bs = f(inp["a_bs"])
    shared = {}
    for nme in ("w_ffn1_up", "w_ffn2_up", "w_ffn1_down", "w_ffn2_down", "w_in", "a_out", "b_out", "c_out", "w_o",
                "w_ple_gate", "w_ple_proj", "c_w", "a_ln_g", "a_ln_b"):
        shared[nme] = f(inp[nme])
    shared["wsT"] = f(a_ws.transpose(0, 1, 3, 2))
    bd = np.zeros((L, 4, 64, 64), np.float32)
    for q in range(16):
        for s in range(4):
            for t in range(4):
                bd[:, :, s * 16 + q, t * 16 + q] = a_ws[:, :, t, s]
    shared["bd"] = bd
    shared["bias4"] = f(a_bs.reshape(L, 4 * 128))
    shared["biass"] = f(np.repeat(a_bs[:, :, 0:4], 16, axis=2).reshape(L, 4 * 64))
    vec = np.zeros((128, L * NV + 8), np.float32)
    col = lambda v, nchunk: f(v).reshape(nchunk, 128).T
    for l in range(L):
        o = l * NV
        vec[:, o + 0:o + 8] = col(inp["g_ffn1"][l], 8)
        vec[:, o + 8:o + 16] = col(inp["g_mix"][l], 8)
        vec[:, o + 16:o + 24] = col(inp["g_ffn2"][l], 8)
        vec[:, o + 24:o + 32] = col(inp["g_ple"][l], 8)
        vec[:, o + 32:o + 36] = col(inp["c_scale"][l], 4)
        for k in range(3):
            vec[:, o + 36 + 4 * k:o + 36 + 4 * k + 4] = col(inp["b_conv"][l, k], 4)
    vec[:, L * NV:L * NV + 8] = col(inp["g_final"], 8)
    shared["vecs"] = vec
    xpr, xsa = f(inp["x_prompt"]), f(inp["x_sample"])
    sc, spl = f(inp["state_conv"]), f(inp["state_pool"])
    ppr, psa = f(inp["p_prompt"]), f(inp["p_sample"])
    maps = []
    for c in range(8):
        sl = slice(16 * c, 16 * c + 16)
        m = dict(shared)
        m["xp"] = xpr[c]
        m["xs"] = f(xsa[sl].transpose(1, 0, 2).reshape(64, D))
        m["sconv"] = f(sc[:, sl].transpose(0, 2, 1, 3).reshape(L, 32, 512))
        m["spool"] = f(spl[:, sl].transpose(0, 2, 1, 3).reshape(L, 240, 512))
        m["pp"] = f(ppr[:, c])
        m["ps"] = f(psa[:, sl].transpose(0, 2, 1, 3).reshape(L, 64, 256))
        maps.append(m)
    return maps


_NC_CACHE = {}


def kernel(**inputs):
    in_maps = _prep_inputs(inputs)
    if "nc" not in _NC_CACHE:
        _NC_CACHE["nc"] = build_nc()
    nc = _NC_CACHE["nc"]
    res = run_bass_kernel_spmd(nc, in_maps, core_ids=list(range(8)))
    r = res.results
    y_prompt = np.stack([r[c]["yp"] for c in range(8)]).astype(np.float32)
    y_sample = np.concatenate([r[c]["ys"].reshape(4, 16, D).transpose(1, 0, 2) for c in range(8)], axis=0).astype(np.float32)
    conv_prompt = np.stack([r[c]["conv_p"] for c in range(8)], axis=1).astype(np.float32)
    conv_sample = np.concatenate([r[c]["conv_s"].reshape(L, 2, 16, 512).transpose(0, 2, 1, 3) for c in range(8)], axis=1).astype(np.float32)
    pool_prompt = np.stack([r[c]["pool_p"] for c in range(8)], axis=1).astype(np.float32)
    pool_sample = np.concatenate([r[c]["pool_s"].reshape(L, 15, 16, 512).transpose(0, 2, 1, 3) for c in range(8)], axis=1).astype(np.float32)
    va = np.concatenate([r[c]["va_s"].reshape(L, 4, 16, 512).transpose(0, 2, 1, 3) for c in range(8)], axis=1).astype(np.float32)
    return (np.ascontiguousarray(y_prompt), np.ascontiguousarray(y_sample), np.ascontiguousarray(conv_prompt),
            np.ascontiguousarray(conv_sample), np.ascontiguousarray(pool_prompt), np.ascontiguousarray(pool_sample),
            np.ascontiguousarray(va))
```
